# Optimizing a Trainium2 kernel written in Bass

```python
import numpy as np
import jax
import jax.numpy as jnp
from jax import lax

D_MODEL = 1024
BATCH = 4
SEQ = 8192
DEPTH = 1

D_RNN = 1344
RNN_BLOCKS = 16
RNN_BLOCK_W = D_RNN // RNN_BLOCKS
RNN_CONV_W = 4
LRU_C = 8.0
N_HEADS = 16
N_KV = 4
HEAD_DIM = 64
GROUP = N_HEADS // N_KV
CMP_LEN = 32
CMP_STRIDE = 16
CMP_HIDDEN = 128
SEL_LEN = 64
N_SELECT = 16
WINDOW = 512
Q_BLOCK = 64
D_FF = 3 * D_MODEL
FFN_CONV_W = 3
EPS = 1e-6
NEG = -1e30
FORCE = 1e6

kernel_name = 'hybrid_rglru_nsa_convffn'


def rms_norm(x, g):
    x32 = x.astype(jnp.float32)
    y = x32 * lax.rsqrt(jnp.mean(x32 * x32, axis=-1, keepdims=True) + EPS)
    return (y * g.astype(jnp.float32)).astype(x.dtype)


def causal_dwconv(x, w, b):
    k = w.shape[0]
    y = lax.conv_general_dilated(x, w[:, None, :].astype(x.dtype), window_strides=(1,),
                                 padding=[(k - 1, 0)], dimension_numbers=('NWC', 'WIO', 'NWC'),
                                 feature_group_count=x.shape[-1])
    return y + b.astype(x.dtype)


def masked_softmax(s, mask):
    p = jax.nn.softmax(jnp.where(mask, s, NEG), axis=-1)
    return p * mask


def rg_lru(x, w_a, b_a, w_x, b_x, lam):
    bsz, s, _ = x.shape
    xb = x.reshape(bsz, s, RNN_BLOCKS, RNN_BLOCK_W)
    r = jax.nn.sigmoid(jnp.einsum('bsnc,ncd->bsnd', xb, w_a).reshape(bsz, s, D_RNN) + b_a)
    i = jax.nn.sigmoid(jnp.einsum('bsnc,ncd->bsnd', xb, w_x).reshape(bsz, s, D_RNN) + b_x)
    log_a = LRU_C * r.astype(jnp.float32) * jax.nn.log_sigmoid(lam.astype(jnp.float32))
    a = jnp.exp(log_a)
    u = jnp.sqrt(-jnp.expm1(2.0 * log_a)) * (i * x).astype(jnp.float32)

    def combine(lhs, rhs):
        a1, b1 = lhs
        a2, b2 = rhs
        return a1 * a2, a2 * b1 + b2

    _, h = lax.associative_scan(combine, (a, u), axis=1)
    return h.astype(x.dtype)


def compress(kv, pe, w1, w2):
    bsz, s, g, hd = kv.shape
    nc = (s - CMP_LEN) // CMP_STRIDE + 1
    pos = np.arange(nc)[:, None] * CMP_STRIDE + np.arange(CMP_LEN)[None, :]
    blocks = kv[:, pos] + pe[None, None, :, None, :]
    blocks = blocks.transpose(0, 3, 1, 2, 4).reshape(bsz, g, nc, CMP_LEN * hd)
    return jax.nn.gelu(blocks @ w1) @ w2


def overlap_matrix(nc, ns):
    i = np.arange(nc)[:, None]
    j = np.arange(ns)[None, :]
    lo = np.maximum(i * CMP_STRIDE, j * SEL_LEN)
    hi = np.minimum(i * CMP_STRIDE + CMP_LEN, (j + 1) * SEL_LEN)
    return jnp.asarray(np.maximum(hi - lo, 0) / CMP_STRIDE, dtype=jnp.float32)


def nsa(q, kc, vc, ks, vs, kw, vw, gates, q_norm_g, k_cmp_norm_g, k_slc_norm_g, k_win_norm_g,
        cmp_pe_k, cmp_w1_k, cmp_w2_k, cmp_pe_v, cmp_w1_v, cmp_w2_v):
    bsz, s, _ = q.shape
    f32 = jnp.float32
    scale = HEAD_DIM ** -0.5

    def heads(t, n):
        return t.reshape(bsz, s, n, HEAD_DIM)

    qh = rms_norm(heads(q, N_HEADS), q_norm_g)
    qh = qh.reshape(bsz, s, N_KV, GROUP, HEAD_DIM).transpose(0, 2, 3, 1, 4)
    k_cmp = rms_norm(compress(heads(kc, N_KV), cmp_pe_k, cmp_w1_k, cmp_w2_k), k_cmp_norm_g)
    v_cmp = compress(heads(vc, N_KV), cmp_pe_v, cmp_w1_v, cmp_w2_v)
    nc = k_cmp.shape[2]
    ns = s // SEL_LEN
    k_slc = rms_norm(heads(ks, N_KV), k_slc_norm_g).transpose(0, 2, 1, 3).reshape(bsz, N_KV, ns, SEL_LEN, HEAD_DIM)
    v_slc = heads(vs, N_KV).transpose(0, 2, 1, 3).reshape(bsz, N_KV, ns, SEL_LEN, HEAD_DIM)
    pad = ((0, 0), (0, 0), (WINDOW, 0), (0, 0))
    k_win = jnp.pad(rms_norm(heads(kw, N_KV), k_win_norm_g).transpose(0, 2, 1, 3), pad)
    v_win = jnp.pad(heads(vw, N_KV).transpose(0, 2, 1, 3), pad)
    g = jax.nn.sigmoid(gates).reshape(bsz, s, N_KV, GROUP, 3).transpose(0, 2, 3, 1, 4)
    overlap = overlap_matrix(nc, ns)
    cmp_end = np.arange(nc) * CMP_STRIDE + CMP_LEN - 1
    n_top = min(N_SELECT, ns)
    gather = jax.vmap(jax.vmap(lambda blk, ix: blk[ix]))
    jb = jnp.arange(ns)

    def block(qi):
        st = qi * Q_BLOCK
        t = st + jnp.arange(Q_BLOCK)
        qb = lax.dynamic_slice_in_dim(qh, st, Q_BLOCK, axis=3)
        gb = lax.dynamic_slice_in_dim(g, st, Q_BLOCK, axis=3)
        sc = jnp.einsum('bgrqd,bgcd->bgrqc', qb, k_cmp).astype(f32) * scale
        p_cmp = masked_softmax(sc, cmp_end[None, :] <= t[:, None])
        o_cmp = jnp.einsum('bgrqc,bgcd->bgrqd', p_cmp.astype(v_cmp.dtype), v_cmp)
        imp = jnp.einsum('bgrqc,cn->bgqn', p_cmp, overlap)
        cur = t // SEL_LEN
        causal_b = jb[None, :] <= cur[:, None]
        forced = (jb[None, :] == 0) | (jb[None, :] == cur[:, None]) | (jb[None, :] == cur[:, None] - 1)
        rank = jnp.where(causal_b, jnp.where(forced, FORCE, imp), -FORCE)
        _, idx = lax.top_k(rank, n_top)
        kg = gather(k_slc, idx)
        vg = gather(v_slc, idx)
        kpos = idx[..., None] * SEL_LEN + jnp.arange(SEL_LEN)
        ok = (idx <= cur[:, None])[..., None] & (kpos <= t[:, None, None])
        sc = jnp.einsum('bgrqd,bgqnld->bgrqnl', qb, kg).astype(f32) * scale
        p = masked_softmax(sc.reshape(bsz, N_KV, GROUP, Q_BLOCK, n_top * SEL_LEN),
                           ok.reshape(bsz, N_KV, 1, Q_BLOCK, n_top * SEL_LEN))
        o_slc = jnp.einsum('bgrqm,bgqmd->bgrqd', p.astype(vg.dtype),
                           vg.reshape(bsz, N_KV, Q_BLOCK, n_top * SEL_LEN, HEAD_DIM))
        kwb = lax.dynamic_slice_in_dim(k_win, st, WINDOW + Q_BLOCK, axis=2)
        vwb = lax.dynamic_slice_in_dim(v_win, st, WINDOW + Q_BLOCK, axis=2)
        kp = st - WINDOW + jnp.arange(WINDOW + Q_BLOCK)
        okw = (kp[None, :] <= t[:, None]) & (kp[None, :] > t[:, None] - WINDOW) & (kp[None, :] >= 0)
        sc = jnp.einsum('bgrqd,bgkd->bgrqk', qb, kwb).astype(f32) * scale
        pw = masked_softmax(sc, okw)
        o_win = jnp.einsum('bgrqk,bgkd->bgrqd', pw.astype(vwb.dtype), vwb)
        return gb[..., 0:1] * o_cmp + gb[..., 1:2] * o_slc + gb[..., 2:3] * o_win

    out = lax.map(block, jnp.arange(s // Q_BLOCK))
    return out.transpose(1, 0, 4, 2, 3, 5).reshape(bsz, s, N_HEADS * HEAD_DIM)


def setup_inputs(seed: int = 0) -> dict:
    key = jax.random.key(seed)
    k = jax.random.split(key, 28)

    def nrm(kk, shape, scale):
        return jax.random.normal(kk, shape, jnp.float32) * scale

    d, hd = D_MODEL, HEAD_DIM
    n_in = 2 * D_RNN + N_HEADS * hd + 6 * N_KV * hd + 3 * N_HEADS + 2 * d
    u = jax.random.uniform(k[9], (D_RNN,), jnp.float32, 0.9, 0.999)
    sg = u ** (1.0 / LRU_C)
    lam = jnp.log(sg) - jnp.log1p(-sg)
    return {
        'x': nrm(k[0], (BATCH, SEQ, d), 1.0),
        'norm1_g': 1.0 + nrm(k[1], (d,), 0.02),
        'w_in': nrm(k[2], (d, n_in), d ** -0.5),
        'rnn_conv_w': nrm(k[3], (RNN_CONV_W, D_RNN), RNN_CONV_W ** -0.5),
        'rnn_conv_b': nrm(k[4], (D_RNN,), 0.02),
        'rg_w_a': nrm(k[5], (RNN_BLOCKS, RNN_BLOCK_W, RNN_BLOCK_W), RNN_BLOCK_W ** -0.5),
        'rg_b_a': nrm(k[6], (D_RNN,), 0.02),
        'rg_w_x': nrm(k[7], (RNN_BLOCKS, RNN_BLOCK_W, RNN_BLOCK_W), RNN_BLOCK_W ** -0.5),
        'rg_b_x': nrm(k[8], (D_RNN,), 0.02),
        'lru_lambda': lam,
        'w_rnn_out': nrm(k[10], (D_RNN, d), D_RNN ** -0.5),
        'q_norm_g': 1.0 + nrm(k[11], (hd,), 0.02),
        'k_cmp_norm_g': 1.0 + nrm(k[12], (hd,), 0.02),
        'k_slc_norm_g': 1.0 + nrm(k[13], (hd,), 0.02),
        'k_win_norm_g': 1.0 + nrm(k[14], (hd,), 0.02),
        'cmp_pe_k': nrm(k[15], (CMP_LEN, hd), 0.1),
        'cmp_w1_k': nrm(k[16], (CMP_LEN * hd, CMP_HIDDEN), (CMP_LEN * hd) ** -0.5),
        'cmp_w2_k': nrm(k[17], (CMP_HIDDEN, hd), CMP_HIDDEN ** -0.5),
        'cmp_pe_v': nrm(k[18], (CMP_LEN, hd), 0.1),
        'cmp_w1_v': nrm(k[19], (CMP_LEN * hd, CMP_HIDDEN), (CMP_LEN * hd) ** -0.5),
        'cmp_w2_v': nrm(k[20], (CMP_HIDDEN, hd), CMP_HIDDEN ** -0.5),
        'w_attn_out': nrm(k[21], (N_HEADS * hd, d), (N_HEADS * hd) ** -0.5),
        'w_out': nrm(k[22], (d, d), d ** -0.5),
        'norm2_g': 1.0 + nrm(k[23], (d,), 0.02),
        'w_up': nrm(k[24], (d, 2 * D_FF), d ** -0.5),
        'ffn_conv_w': nrm(k[25], (FFN_CONV_W, 2 * D_FF), FFN_CONV_W ** -0.5),
        'ffn_conv_b': nrm(k[26], (2 * D_FF,), 0.02),
        'w_down': nrm(k[27], (D_FF, d), D_FF ** -0.5),
    }


def reference(x, norm1_g, w_in, rnn_conv_w, rnn_conv_b, rg_w_a, rg_b_a, rg_w_x, rg_b_x, lru_lambda,
              w_rnn_out, q_norm_g, k_cmp_norm_g, k_slc_norm_g, k_win_norm_g, cmp_pe_k, cmp_w1_k, cmp_w2_k,
              cmp_pe_v, cmp_w1_v, cmp_w2_v, w_attn_out, w_out, norm2_g, w_up, ffn_conv_w, ffn_conv_b, w_down):
    kvw = N_KV * HEAD_DIM
    sizes = [D_RNN, D_RNN, N_HEADS * HEAD_DIM] + [kvw] * 6 + [3 * N_HEADS, D_MODEL, D_MODEL]
    splits = np.cumsum(sizes)[:-1].tolist()
    for _ in range(DEPTH):
        h = rms_norm(x, norm1_g)
        proj = h @ w_in
        (rx, rgate, q, kc, vc, ks, vs, kw, vw, nsa_g, mg_rnn, mg_attn) = jnp.split(proj, splits, axis=-1)
        hr = rg_lru(causal_dwconv(rx, rnn_conv_w, rnn_conv_b), rg_w_a, rg_b_a, rg_w_x, rg_b_x, lru_lambda)
        y_rnn = (hr * jax.nn.gelu(rgate)) @ w_rnn_out
        o = nsa(q, kc, vc, ks, vs, kw, vw, nsa_g, q_norm_g, k_cmp_norm_g, k_slc_norm_g, k_win_norm_g,
                cmp_pe_k, cmp_w1_k, cmp_w2_k, cmp_pe_v, cmp_w1_v, cmp_w2_v)
        y_attn = o @ w_attn_out
        x = x + (jax.nn.sigmoid(mg_rnn) * y_rnn + jax.nn.sigmoid(mg_attn) * y_attn) @ w_out
        up = causal_dwconv(rms_norm(x, norm2_g) @ w_up, ffn_conv_w, ffn_conv_b)
        gate, val = jnp.split(up, 2, axis=-1)
        x = x + (jax.nn.gelu(gate) * val) @ w_down
    return x
```

```python
import numpy as np
import ml_dtypes
from contextlib import ExitStack
import concourse.bass as bass
import concourse.mybir as mybir
from concourse.bass_utils import run_bass_kernel_spmd

F32 = mybir.dt.float32
BF16 = mybir.dt.bfloat16
AF = mybir.ActivationFunctionType
ALU = mybir.AluOpType

D = 1024
KD = 8
D_RNN = 1344
NB = 16
BW = 84
HD = 64
NH = 16
NKV = 4
D_FF = 3072
EPS = 1e-6
C_RX, C_RG, C_Q, C_KC, C_VC, C_KS, C_KW, C_VS, C_VW, C_NG, C_MR, C_MA = (
    0, 1344, 2688, 3712, 3968, 4224, 4480, 4736, 4992, 5248, 5296, 6320)
N_IN = 7344


import os


class _Stop(Exception):
    pass


_STOPCTX = {}


def chk(tag):
    if os.environ.get('P1STOP') == tag:
        _STOPCTX['S'].barrier()
        _STOPCTX['S'].emit()
        raise _Stop()


class _Rec:
    def __init__(self):
        self.call = None

    def __getattr__(self, name):
        def f(*a, **k):
            self.call = (name, a, k)
            return self
        return f


class Sched:
    CE = ['pe', 'act', 'dve', 'pool']

    def __init__(self, nc, es):
        self.nc = nc
        self.es = es
        self.gen = 0
        self.dfree = {'sp': [], 'pool': []}
        self.ndsem = 0
        self.sem = {e: es.enter_context(nc.semaphore("sem_" + e)) for e in self.CE}
        self.cnt = {e: 0 for e in self.CE}
        self.ops = {e: [] for e in self.CE + ['sp']}
        self.waited = {e: {} for e in self.CE + ['sp']}
        self.res = {}
        self.dsem = {}
        self.semobj = {}
        for e in self.CE:
            self.semobj[('c', e, 0)] = self.sem[e]
        self.nops = 0

    def _res(self, name):
        r = self.res.get(name)
        if r is None:
            r = {'w': None, 'r': {}}
            self.res[name] = r
        return r

    def op(self, eng, fn, reads=(), writes=(), dma=None):
        need = {}

        def add(ev, kind):
            if ev is None:
                return
            sk, val, src = ev
            if dma is None and src == eng and kind != 'raw' and eng == 'pe':
                return
            if need.get(sk, 0) < val:
                need[sk] = val
        for r in reads:
            add(self._res(r)['w'], 'raw')
        for w in writes:
            rr = self._res(w)
            add(rr['w'], 'waw')
            for ev in rr['r'].values():
                add(ev, 'war')
        if dma is not None:
            if dma not in self.dsem:
                if self.dfree[eng]:
                    s, c0 = self.dfree[eng].pop()
                else:
                    self.ndsem += 1
                    s, c0 = self.es.enter_context(self.nc.semaphore("dsem%d" % self.ndsem)), 0
                self.dsem[dma] = [s, c0, c0, eng]
                self.semobj[('d', dma)] = s
            d = self.dsem[dma]
            if d[1] > d[2]:
                add((('d', dma), d[1], 'dmaq'), 'raw')
        waits = []
        wd = self.waited[eng]
        for sk, val in need.items():
            if wd.get(sk, 0) >= val:
                continue
            wd[sk] = val
            waits.append((self.semobj[sk], val))
        if dma is not None:
            d = self.dsem[dma]
            d[1] += 16
            ev = (('d', dma), d[1], 'dmaq')
            inc = (d[0], 16)
        else:
            self.cnt[eng] += 1
            ev = (('c', eng, self.gen), self.cnt[eng], eng)
            inc = (self.sem[eng], 1)
        rec = _Rec()
        fn(rec)
        assert rec.call is not None
        self.ops[eng].append((waits, rec.call, inc))
        self.nops += 1
        for w in writes:
            self.res[w] = {'w': ev, 'r': {}}
        for r in reads:
            self._res(r)['r'][ev[0]] = ev
        return ev

    def barrier(self, fresh=True):
        evs = [(('c', e, self.gen), self.cnt[e]) for e in self.CE if self.cnt[e] > 0]
        evs += [(('d', k), d[1]) for k, d in self.dsem.items() if d[1] > d[2]]
        for e in self.CE + ['sp']:
            waits = []
            wd = self.waited[e]
            for sk, val in evs:
                if sk == ('c', e, self.gen):
                    continue
                if wd.get(sk, 0) >= val:
                    continue
                wd[sk] = val
                waits.append((self.semobj[sk], val))
            if waits:
                self.ops[e].append((waits, None, None))
        self.res = {}
        for k, d in self.dsem.items():
            self.dfree[d[3]].append((d[0], d[1]))
        self.dsem = {}
        for e in self.waited:
            self.waited[e] = {k: v for k, v in self.waited[e].items() if k[0] != 'd'}
        if not fresh:
            return
        self.gen += 1
        for e in self.CE:
            self.sem[e] = self.es.enter_context(self.nc.semaphore("sem_%s_g%d" % (e, self.gen)))
            self.cnt[e] = 0
            self.semobj[('c', e, self.gen)] = self.sem[e]

    def emit(self):
        block = self.es.enter_context(self.nc.Block())

        def run(engobj, lst):
            for waits, fn, inc in lst:
                for s, v in waits:
                    engobj.wait_ge(s, v)
                if fn is None:
                    continue
                name, a, k = fn
                ins = getattr(engobj, name)(*a, **k)
                ins.then_inc(inc[0], inc[1])

        @block.tensor
        def _(e):
            run(e, self.ops['pe'])

        @block.scalar
        def _(e):
            run(e, self.ops['act'])

        @block.vector
        def _(e):
            run(e, self.ops['dve'])

        @block.gpsimd
        def _(e):
            run(e, self.ops['pool'])

        @block.sync
        def _(e):
            run(e, self.ops['sp'])


def make_cfg(L, QS):
    assert L % 512 == 0 and QS % 512 == 0 and QS - 128 >= 512
    cfg = dict(L=L, QS=QS, LQ=L - QS + 128, NS=L // 64, NCB=L // 16 - 1,
               NCC=(L // 16 - 1 + 127) // 128, NQT=(L - QS + 128) // 128)
    tiles = []
    t = 0
    while t < QS - 128:
        w = min(512, QS - 128 - t)
        tiles.append((t, w, 'K'))
        t += w
    tiles.append((QS - 128, 128, 'H'))
    t = QS
    while t < L:
        tiles.append((t, 512, 'Q'))
        t += 512
    cfg['tiles'] = tiles
    return cfg


WEIGHTS = [
    ('win', 128, 8, N_IN), ('wro', BW, NB, D), ('wao', 128, 8, D), ('wout', 128, 8, D),
    ('wup', 128, 8, 2 * D_FF), ('wdn', 128, 24, D), ('rga', BW, NB, BW), ('rgx', BW, NB, BW),
    ('w1k', 128, 32, 128), ('w1v', 128, 32, 128), ('w2k', 128, 1, 64), ('w2v', 128, 1, 64),
    ('pek', 128, 1, 32), ('pev', 128, 1, 32),
]


def build_program(cfg, debug=False, upto=3):
    try:
        return _build_program(cfg, debug, upto)
    except _Stop:
        return _STOPCTX['nc']


def _build_program(cfg, debug=False, upto=3):
    L, QS, LQ, NS, NCC, NQT = cfg['L'], cfg['QS'], cfg['LQ'], cfg['NS'], cfg['NCC'], cfg['NQT']
    NCH = L // 128
    nc = bass.Bass("TRN2", target_bir_lowering=False)
    ins = {}

    def IN(name, shape, dt=F32):
        ins[name] = nc.dram_tensor(name, list(shape), dt, kind="ExternalInput").ap()
        return ins[name]

    skind = "ExternalOutput" if debug else "Internal"

    def SCR(name, shape, dt=BF16):
        return nc.dram_tensor(name, list(shape), dt, kind=skind).ap()

    xloc = IN('xloc', [L, D])
    wsrc = {n: IN('f_' + n, [kp, kc, N]) for n, kp, kc, N in WEIGHTS}
    g1 = IN('g1', [128, 8])
    g2 = IN('g2', [128, 8])
    rnnc_d = IN('rnnc', [BW, NB, 8])
    ffnc_d = IN('ffnc', [128, 48, 4])
    hg_d = IN('hgains', [128, 4])
    flags_d = IN('flags', [128, 2])
    rankc_d = IN('rankc', [NQT, 128, 2 * NS])
    ovl_d = IN('ovl', [128, NCC, NS], BF16)
    validc_d = IN('validc', [128, NCC])
    cmask_d = IN('cmask', [128, 17, 128], BF16)
    tri_d = IN('tri', [128, 2, 128], BF16)
    wexp_d = IN('wexp', [NS, L], BF16)
    ident_d = IN('ident', [128, 128], BF16)
    bones_d = IN('bones', [128, 128], BF16)
    y = nc.dram_tensor('y', [L - QS, D], F32, kind="ExternalOutput").ap()

    wscr = {n: SCR('b_' + n, [kp, kc, N]) for n, kp, kc, N in WEIGHTS}
    KST = SCR('KST', [128, 2, L])
    KWT = SCR('KWT', [128, 2, L])
    VS = SCR('VS', [L, 4, 65])
    VW = SCR('VW', [L, 4, 65])
    KCT = SCR('KCT', [128, 2, NCC * 128])
    VCM = SCR('VCM', [NCC * 128, 4, 64])
    QT = SCR('QT', [128, 8, LQ])
    GATES = SCR('GATES', [LQ, 48], F32)
    T1T = SCR('T1T', [128, 8, LQ])
    SGAT = SCR('SGAT', [128, 8, LQ])
    OT = SCR('OT', [128, 8, LQ])

    with ExitStack() as es0:
        S = Sched(nc, es0)
        _STOPCTX['S'] = S
        _STOPCTX['nc'] = nc
        uid = [0]

        def PE(fn, r, w):
            return S.op('pe', fn, r, w)

        def ACT(fn, r, w):
            return S.op('act', fn, r, w)

        def DVE(fn, r, w):
            return S.op('dve', fn, r, w)

        def POOL(fn, r, w):
            return S.op('pool', fn, r, w)

        def LOAD(out, in_, r, w, key):
            return S.op('sp', lambda e: e.dma_start(out=out, in_=in_), r, w, dma=key)

        def STORE(out, in_, r, w, key):
            return S.op('pool', lambda e: e.dma_start(out=out, in_=in_), r, w, dma=key)

        class WStream:
            def __init__(self, es, plan, nslots=4, tag='ws'):
                self.plan = plan
                self.slots = [es.enter_context(nc.sbuf_tensor("ws_%s_slot%d" % (tag, i), [128, 4096], BF16))
                              for i in range(nslots)]
                self.tag = tag
                self.issued = 0
                self.taken = 0
                self.ns = nslots

            def _issue(self):
                i = self.issued
                name, c0, n = self.plan[i]
                _, kp, kc, N = [wt for wt in WEIGHTS if wt[0] == name][0]
                sl = i % self.ns
                view = self.slots[sl][0:kp, 0:kc * n].rearrange("p (k n) -> p k n", k=kc)
                LOAD(view, wscr[name][:, :, c0:c0 + n], ['b_' + name], ['%s%d' % (self.tag, sl)],
                     '%s%d' % (self.tag, sl))
                self.issued += 1

            def next(self, expect=None):
                while self.issued < min(len(self.plan), self.taken + self.ns - 1):
                    self._issue()
                i = self.taken
                name, c0, n = self.plan[i]
                if expect is not None:
                    assert expect == (name, c0), (expect, self.plan[i])
                _, kp, kc, N = [wt for wt in WEIGHTS if wt[0] == name][0]
                sl = i % self.ns
                self.taken += 1
                view = self.slots[sl][0:kp, 0:kc * n].rearrange("p (k n) -> p k n", k=kc)
                return view, '%s%d' % (self.tag, sl)

        with ExitStack() as es:
            def sb(name, shape, dt=F32):
                return es.enter_context(nc.sbuf_tensor('s0_' + name, list(shape), dt))
            stf = [sb('p0f%d' % i, [128, 4096]) for i in range(2)]
            stb = [sb('p0b%d' % i, [128, 4096], BF16) for i in range(2)]
            g12 = sb('g12', [128, 16])
            LOAD(g12[:, 0:8], g1[:, :], [], ['g12'], 'g12a')
            LOAD(g12[:, 8:16], g2[:, :], [], ['g12'], 'g12b')
            blk = 0
            for name, kp, kc, N in WEIGHTS:
                cb = min(N, 4096 // kc)
                c0 = 0
                while c0 < N:
                    n = min(cb, N - c0)
                    sl = blk % 2
                    fv = stf[sl][0:kp, 0:kc * n].rearrange("p (k n) -> p k n", k=kc)
                    bv = stb[sl][0:kp, 0:kc * n].rearrange("p (k n) -> p k n", k=kc)
                    LOAD(fv, wsrc[name][:, :, c0:c0 + n], [], ['p0f%d' % sl], 'p0f%d' % sl)
                    if name in ('win', 'wup'):
                        goff = 0 if name == 'win' else 8
                        for k in range(kc):
                            if k % 2 == 0:
                                DVE(lambda e, k=k, fv=fv, bv=bv, goff=goff: e.tensor_scalar(
                                    out=bv[:, k, :], in0=fv[:, k, :], scalar1=g12[:, goff + k:goff + k + 1],
                                    scalar2=None, op0=ALU.mult), ['p0f%d' % sl, 'g12'], ['p0b%d' % sl])
                            else:
                                ACT(lambda e, k=k, fv=fv, bv=bv, goff=goff: e.activation(
                                    out=bv[:, k, :], in_=fv[:, k, :], func=AF.Copy,
                                    scale=g12[:, goff + k:goff + k + 1]), ['p0f%d' % sl, 'g12'], ['p0b%d' % sl])
                    else:
                        if blk % 2 == 0:
                            DVE(lambda e, fv=fv, bv=bv: e.tensor_copy(out=bv, in_=fv), ['p0f%d' % sl], ['p0b%d' % sl])
                        else:
                            ACT(lambda e, fv=fv, bv=bv: e.copy(out=bv, in_=fv), ['p0f%d' % sl], ['p0b%d' % sl])
                    LOAD(wscr[name][:, :, c0:c0 + n], bv, ['p0b%d' % sl], ['b_' + name], 'p0s%d' % sl)
                    c0 += n
                    blk += 1
            S.barrier()

        if upto == 0:
            S.emit()
            return nc
        with ExitStack() as es:
            def sb(name, shape, dt=F32):
                return es.enter_context(nc.sbuf_tensor('s1_' + name, list(shape), dt))

            def ps(name, dt=F32):
                return es.enter_context(nc.psum_tensor('p0_' + name, [128, 512 if dt == F32 else 1024], dt))
            plan = []
            for (t0, W, mode) in cfg['tiles']:
                q = mode != 'K'
                for a in range(4):
                    plan.append(('win', C_RX + 336 * a, 336))
                    if q:
                        plan.append(('win', C_RG + 336 * a, 336))
                plan += [('win', C_KC, 512), ('win', C_KS, 512), ('win', C_VS, 512)]
                if q:
                    plan += [('win', C_Q, 512), ('win', C_Q + 512, 512), ('win', C_NG, 48)]
                    for jj in range(4):
                        plan += [('win', C_MR + 256 * jj, 256), ('wro', 256 * jj, 256)]
                    plan += [('win', C_MA, 512), ('win', C_MA + 512, 512)]
            ws = WStream(es, plan, 4, 'w1s')
            ident = sb('ident', [128, 128], BF16)
            bones = sb('bones', [128, 128], BF16)
            rga = sb('rga', [BW, NB, BW], BF16)
            rgx = sb('rgx', [BW, NB, BW], BF16)
            w1k = sb('w1k', [128, 32, 128], BF16)
            w1v = sb('w1v', [128, 32, 128], BF16)
            w2k = sb('w2k', [128, 64], BF16)
            w2v = sb('w2v', [128, 64], BF16)
            pek = sb('pek', [128, 32], BF16)
            pev = sb('pev', [128, 32], BF16)
            rnnc = sb('rnnc', [BW, NB, 8])
            hgn = sb('hgn', [128, 4])
            flags = sb('flags', [128, 2])
            LOAD(ident[:], ident_d[:, :], [], ['ident'], 'c_ident')
            LOAD(bones[:], bones_d[:, :], [], ['bones'], 'c_bones')
            LOAD(rga[:], wscr['rga'][:, :, :], ['b_rga'], ['rga'], 'c_rga')
            LOAD(rgx[:], wscr['rgx'][:, :, :], ['b_rgx'], ['rgx'], 'c_rgx')
            LOAD(w1k[:], wscr['w1k'][:, :, :], ['b_w1k'], ['w1k'], 'c_w1k')
            LOAD(w1v[:], wscr['w1v'][:, :, :], ['b_w1v'], ['w1v'], 'c_w1v')
            LOAD(w2k[:], wscr['w2k'][:, 0, :], ['b_w2k'], ['w2k'], 'c_w2k')
            LOAD(w2v[:], wscr['w2v'][:, 0, :], ['b_w2v'], ['w2v'], 'c_w2v')
            LOAD(pek[:], wscr['pek'][:, 0, :], ['b_pek'], ['pek'], 'c_pek')
            LOAD(pev[:], wscr['pev'][:, 0, :], ['b_pev'], ['pev'], 'c_pev')
            LOAD(rnnc[:], rnnc_d[:, :, :], [], ['rnnc'], 'c_rnnc')
            LOAD(hgn[:], hg_d[:, :], [], ['hgn'], 'c_hgn')
            LOAD(flags[:], flags_d[:, :], [], ['flags'], 'c_flags')
            state = sb('state', [BW, NB])
            rxhalo = sb('rxhalo', [BW, NB, 3])
            kchalo = sb('kchalo', [128, 4, 16], BF16)
            lcn = sb('lcn', [BW, NB, 2])
            DVE(lambda e: e.memset(state[:], 0.0), [], ['state'])
            DVE(lambda e: e.memset(rxhalo[:], 0.0), [], ['rxhalo'])
            DVE(lambda e: e.memset(kchalo[:], 0.0), [], ['kchalo'])
            tl = [sb('tl%d' % i, [BW, NB]) for i in range(4)]
            ACT(lambda e: e.activation(out=tl[0][:], in_=rnnc[:, :, 7], func=AF.Exp, scale=-1.0), ['rnnc'], ['tl0'])
            DVE(lambda e: e.tensor_scalar(out=tl[1][:], in0=tl[0][:], scalar1=2.0, scalar2=None, op0=ALU.add), ['tl0'], ['tl1'])
            DVE(lambda e: e.reciprocal(out=tl[2][:], in_=tl[1][:]), ['tl1'], ['tl2'])
            DVE(lambda e: e.tensor_tensor(out=tl[1][:], in0=tl[0][:], in1=tl[2][:], op=ALU.mult), ['tl0', 'tl2'], ['tl1'])
            DVE(lambda e: e.tensor_tensor(out=tl[2][:], in0=tl[1][:], in1=tl[1][:], op=ALU.mult), ['tl1'], ['tl2'])
            DVE(lambda e: e.tensor_scalar(out=tl[3][:], in0=tl[2][:], scalar1=1.0 / 9, scalar2=1.0 / 7, op0=ALU.mult, op1=ALU.add), ['tl2'], ['tl3'])
            for cst in (1.0 / 5, 1.0 / 3, 1.0):
                DVE(lambda e: e.tensor_tensor(out=tl[3][:], in0=tl[3][:], in1=tl[2][:], op=ALU.mult), ['tl3', 'tl2'], ['tl3'])
                DVE(lambda e, cst=cst: e.tensor_scalar(out=tl[3][:], in0=tl[3][:], scalar1=cst, scalar2=None, op0=ALU.add), ['tl3'], ['tl3'])
            DVE(lambda e: e.tensor_tensor(out=tl[3][:], in0=tl[3][:], in1=tl[1][:], op=ALU.mult), ['tl3', 'tl1'], ['tl3'])
            DVE(lambda e: e.tensor_scalar(out=lcn[:, :, 0], in0=tl[3][:], scalar1=-16.0, scalar2=None, op0=ALU.mult), ['tl3'], ['lcn'])

            A = [ps('A%d' % i) for i in range(3)]
            G = [ps('G%d' % i) for i in range(2)]
            Hh = ps('Hh')
            Hh2 = ps('Hh2')
            Tb = ps('Tb', BF16)
            acc_i = [0]

            def nextA():
                i = acc_i[0] % 3
                acc_i[0] += 1
                return A[i], 'A%d' % i
            hbias = sb('hbias', [128, 2])
            for kv, (w1, pe_) in enumerate(((w1k, pek), (w1v, pev))):
                for l in range(32):
                    PE(lambda e, l=l, w1=w1, pe_=pe_, kv=kv: e.matmul(Hh[:, kv:kv + 1], lhsT=w1[0:64, l, :], rhs=pe_[0:64, l:l + 1],
                                                                      start=(l == 0 and kv == 0), stop=(l == 31), skip_group_check=True),
                       ['w1k', 'w1v', 'pek', 'pev'], ['Hh'])
            ACT(lambda e: e.copy(out=hbias[:], in_=Hh[:, 0:2]), ['Hh'], ['hbias'])

            xt = [sb('xt%d' % i, [128, 4, D]) for i in range(2)]
            junk = sb('junk', [128, D], BF16)
            xn = sb('xn', [128, D], BF16)
            ssq = sb('ssq', [128, 8])
            hT = sb('hT', [128, 8, 512], BF16)
            hgT = sb('hgT', [BW, NB, 512], BF16)
            rxb = [sb('rxb%d' % i, [BW, 515]) for i in range(2)]
            xc = [sb('xc%d' % i, [BW, 512]) for i in range(2)]
            xcb = [sb('xcb%d' % i, [BW, 512], BF16) for i in range(2)]
            rr = [sb('rr%d' % i, [BW, 512]) for i in range(2)]
            ii = [sb('ii%d' % i, [BW, 512]) for i in range(2)]
            aa = [sb('aa%d' % i, [BW, 512]) for i in range(2)]
            mm_ = [sb('mm%d' % i, [BW, 512]) for i in range(2)]
            hh = [sb('hh%d' % i, [BW, 512]) for i in range(2)]
            kcT = sb('kcT', [128, 4, 528], BF16)
            sqb = sb('sqb', [128, 512], BF16)
            rstd = sb('rstd', [128, 512])
            stg = [sb('stg%d' % i, [128, 512], BF16) for i in range(4)]
            vstg = [sb('vstg%d' % i, [128, 2, 4, 65], BF16) for i in range(2)]
            gstg = [sb('gstg%d' % i, [128, 48]) for i in range(2)]
            hid = sb('hid', [128, 8, 32], BF16)
            vcs = sb('vcs', [32, 4, 64], BF16)
            sgb = sb('sgb', [128, 512])
            stg_i = [0]
            for i in range(2):
                DVE(lambda e, i=i: e.memset(vstg[i][:], 1.0), [], ['vstg%d' % i])

            def next_stg():
                i = stg_i[0] % 4
                stg_i[0] += 1
                return stg[i], 'stg%d' % i

            def load_x(ti):
                t0, W, mode = cfg['tiles'][ti]
                nsub = W // 128
                LOAD(xt[ti % 2][:, 0:nsub, :], xloc[t0:t0 + W, :].rearrange("(s p) d -> p s d", p=128),
                     [], ['xt%d' % (ti % 2)], 'xt%d' % (ti % 2))

            def rmsnorm_T(xtile, xname, nsub, dstT, dname):
                for s in range(nsub):
                    ACT(lambda e, s=s: e.activation(out=junk[:], in_=xtile[:, s, :], func=AF.Square, accum_out=ssq[:, s:s + 1]),
                        [xname], ['junk', 'ssq'])
                ACT(lambda e: e.activation(out=ssq[:, 4:4 + nsub], in_=ssq[:, 0:nsub], func=AF.Sqrt, scale=1.0 / D, bias=EPS), ['ssq'], ['ssq'])
                DVE(lambda e: e.reciprocal(out=ssq[:, 4:4 + nsub], in_=ssq[:, 4:4 + nsub]), ['ssq'], ['ssq'])
                for s in range(nsub):
                    ACT(lambda e, s=s: e.activation(out=xn[:], in_=xtile[:, s, :], func=AF.Copy, scale=ssq[:, 4 + s:5 + s]),
                        [xname, 'ssq'], ['xn'])
                    for k in range(8):
                        PE(lambda e, k=k: e.transpose(out=Tb[:, k * 128:(k + 1) * 128], in_=xn[:, k * 128:(k + 1) * 128], identity=ident[:]),
                           ['xn', 'ident'], ['Tb'])
                    DVE(lambda e, s=s: e.tensor_copy(out=dstT[:, :, s * 128:(s + 1) * 128],
                                                     in_=Tb[:, :].rearrange("p (k q) -> p k q", k=8)), ['Tb'], [dname])

            def headnorm(src_ps, src_name, W, gcol, dst, dname, extra_reads=()):
                ACT(lambda e: e.activation(out=sqb[:, :W], in_=src_ps[:, :W], func=AF.Square), [src_name], ['sqb'])
                PE(lambda e: e.matmul(G[1][:, :W], lhsT=bones[:], rhs=sqb[:, :W], start=True, stop=True), ['sqb', 'bones'], ['G1'])
                ACT(lambda e: e.activation(out=rstd[:, :W], in_=G[1][:, :W], func=AF.Sqrt, bias=EPS), ['G1'], ['rstd'])
                DVE(lambda e: e.reciprocal(out=rstd[:, :W], in_=rstd[:, :W]), ['rstd'], ['rstd'])
                DVE(lambda e: e.scalar_tensor_tensor(out=dst, in0=src_ps[:, :W], scalar=hgn[:, gcol:gcol + 1], in1=rstd[:, :W],
                                                     op0=ALU.mult, op1=ALU.mult), [src_name, 'rstd', 'hgn'], [dname])

            load_x(0)
            chk('setup')
            for ti, (t0, W, mode) in enumerate(cfg['tiles']):
                qside = mode != 'K'
                nsub = W // 128
                tq0 = t0 - (QS - 128)
                if ti + 1 < len(cfg['tiles']):
                    load_x(ti + 1)
                xtile, xname = xt[ti % 2], 'xt%d' % (ti % 2)
                rmsnorm_T(xtile, xname, nsub, hT, 'hT')
                chk('norm')
                if t0 == QS:
                    DVE(lambda e: e.tensor_scalar(out=state[:], in0=state[:], scalar1=flags[0:BW, 1:2], scalar2=None, op0=ALU.mult),
                        ['state', 'flags'], ['state'])
                for a in range(4):
                    wrx, wrxn = ws.next(('win', C_RX + 336 * a))
                    if qside:
                        wrg, wrgn = ws.next(('win', C_RG + 336 * a))
                    for cl in range(4):
                        c = 4 * a + cl
                        p = c % 2
                        pa, pan = nextA()
                        for k in range(8):
                            PE(lambda e, k=k, pa=pa, wrx=wrx, cl=cl: e.matmul(pa[0:BW, :W], lhsT=wrx[:, k, cl * BW:(cl + 1) * BW], rhs=hT[:, k, :W],
                                                                             start=(k == 0), stop=(k == 7)), [wrxn, 'hT'], [pan])
                        rb, rbn = rxb[p], 'rxb%d' % p
                        DVE(lambda e, rb=rb, c=c: e.tensor_copy(out=rb[:, 0:3], in_=rxhalo[:, c, :]), ['rxhalo'], [rbn])
                        ACT(lambda e, rb=rb, pa=pa: e.copy(out=rb[:, 3:3 + W], in_=pa[0:BW, :W]), [pan], [rbn])
                        ACT(lambda e, rb=rb, c=c: e.copy(out=rxhalo[:, c, :], in_=rb[:, W:W + 3]), [rbn], ['rxhalo'])
                        xcc, xcn = xc[p], 'xc%d' % p
                        ACT(lambda e, pa=pa, xcc=xcc, c=c: e.activation(out=xcc[:, :W], in_=pa[0:BW, :W], func=AF.Identity,
                                                                      scale=rnnc[:, c, 3:4], bias=rnnc[:, c, 4:5]), [pan, 'rnnc'], [xcn])
                        for k in range(3):
                            DVE(lambda e, k=k, rb=rb, xcc=xcc, c=c: e.scalar_tensor_tensor(
                                out=xcc[:, :W], in0=rb[:, k:k + W], scalar=rnnc[:, c, k:k + 1], in1=xcc[:, :W],
                                op0=ALU.mult, op1=ALU.add), [rbn, xcn, 'rnnc'], [xcn])
                        xb_, xbn = xcb[p], 'xcb%d' % p
                        ACT(lambda e, xb_=xb_, xcc=xcc: e.copy(out=xb_[:, :W], in_=xcc[:, :W]), [xcn], [xbn])
                        PE(lambda e, c=c, xb_=xb_: e.matmul(G[0][0:BW, :W], lhsT=rga[:, c, :], rhs=xb_[:, :W], start=True, stop=True), ['rga', xbn], ['G0'])
                        PE(lambda e, c=c, xb_=xb_: e.matmul(G[1][0:BW, :W], lhsT=rgx[:, c, :], rhs=xb_[:, :W], start=True, stop=True), ['rgx', xbn], ['G1'])
                        r_, rn = rr[p], 'rr%d' % p
                        i_, in_n = ii[p], 'ii%d' % p
                        ACT(lambda e, r_=r_, c=c: e.activation(out=r_[:, :W], in_=G[0][0:BW, :W], func=AF.Sigmoid, bias=rnnc[:, c, 5:6]), ['G0', 'rnnc'], [rn])
                        ACT(lambda e, i_=i_, c=c: e.activation(out=i_[:, :W], in_=G[1][0:BW, :W], func=AF.Sigmoid, bias=rnnc[:, c, 6:7]), ['G1', 'rnnc'], [in_n])
                        a_, an = aa[p], 'aa%d' % p
                        m_, mn = mm_[p], 'mm%d' % p
                        ACT(lambda e, a_=a_, r_=r_, c=c: e.activation(out=a_[:, :W], in_=r_[:, :W], func=AF.Exp, scale=lcn[:, c, 0:1]), [rn, 'lcn'], [an])
                        DVE(lambda e, a_=a_, m_=m_: e.tensor_tensor(out=m_[:, :W], in0=a_[:, :W], in1=a_[:, :W], op=ALU.mult), [an], [mn])
                        ACT(lambda e, m_=m_: e.activation(out=m_[:, :W], in_=m_[:, :W], func=AF.Sqrt, scale=-1.0, bias=1.0), [mn], [mn])
                        DVE(lambda e, i_=i_, xcc=xcc: e.tensor_tensor(out=i_[:, :W], in0=i_[:, :W], in1=xcc[:, :W], op=ALU.mult), [in_n, xcn], [in_n])
                        DVE(lambda e, i_=i_, m_=m_: e.tensor_tensor(out=i_[:, :W], in0=i_[:, :W], in1=m_[:, :W], op=ALU.mult), [in_n, mn], [in_n])
                        h_, hn = hh[p], 'hh%d' % p
                        DVE(lambda e, h_=h_, a_=a_, i_=i_, c=c: e.tensor_tensor_scan(out=h_[:, :W], data0=a_[:, :W], data1=i_[:, :W],
                                                                                     initial=state[:, c:c + 1], op0=ALU.mult, op1=ALU.add),
                            [an, in_n, 'state'], [hn])
                        ACT(lambda e, h_=h_, c=c: e.copy(out=state[:, c:c + 1], in_=h_[:, W - 1:W]), [hn], ['state'])
                        if qside:
                            pg, pgn = nextA()
                            for k in range(8):
                                PE(lambda e, k=k, pg=pg, wrg=wrg, cl=cl: e.matmul(pg[0:BW, :W], lhsT=wrg[:, k, cl * BW:(cl + 1) * BW], rhs=hT[:, k, :W],
                                                                                 start=(k == 0), stop=(k == 7)), [wrgn, 'hT'], [pgn])
                            ACT(lambda e, pg=pg, r_=r_: e.activation(out=r_[:, :W], in_=pg[0:BW, :W], func=AF.Gelu_apprx_tanh), [pgn], [rn])
                            DVE(lambda e, h_=h_, r_=r_, c=c: e.tensor_tensor(out=hgT[:, c, :W], in0=h_[:, :W], in1=r_[:, :W], op=ALU.mult), [hn, rn], ['hgT'])
                chk('rnn')
                wkc, wkcn = ws.next(('win', C_KC))
                nblk = W // 16
                for idx in range(4):
                    pa, pan = nextA()
                    for k in range(8):
                        PE(lambda e, k=k, pa=pa, idx=idx: e.matmul(pa[:, :W], lhsT=wkc[:, k, idx * 128:(idx + 1) * 128], rhs=hT[:, k, :W],
                                                                   start=(k == 0), stop=(k == 7)), [wkcn, 'hT'], [pan])
                    DVE(lambda e, idx=idx: e.tensor_copy(out=kcT[:, idx, 0:16], in_=kchalo[:, idx, :]), ['kchalo'], ['kcT'])
                    ACT(lambda e, pa=pa, idx=idx: e.copy(out=kcT[:, idx, 16:16 + W], in_=pa[:, :W]), [pan], ['kcT'])
                    ACT(lambda e, idx=idx: e.copy(out=kchalo[:, idx, :], in_=kcT[:, idx, W:W + 16]), ['kcT'], ['kchalo'])
                chk('cmp1')
                for kv in range(2):
                    w1 = w1k if kv == 0 else w1v
                    for hf in range(2):
                        HB, hbn = (Hh, 'Hh') if hf == 0 else (Hh2, 'Hh2')
                        for G_ in range(2):
                            base = kcT[64 * hf:64 * hf + 64, kv * 2 + G_, 0:1]
                            for l in range(32):
                                rhs = bass.AP(base.tensor, base.offset + l, [list(base.ap[0]), [16, nblk]])
                                PE(lambda e, l=l, rhs=rhs, w1=w1, hf=hf, G_=G_, HB=HB: e.matmul(
                                    HB[:, G_ * 32:G_ * 32 + nblk], lhsT=w1[64 * hf:64 * hf + 64, l, :], rhs=rhs,
                                    start=(l == 0 and G_ == 0), stop=(l == 31), skip_group_check=True),
                                   ['kcT', 'w1k', 'w1v'], [hbn])
                    for hf in range(2):
                        HB, hbn = (Hh, 'Hh') if hf == 0 else (Hh2, 'Hh2')
                        ACT(lambda e, kv=kv, hf=hf, HB=HB: e.activation(
                            out=hid[:, kv * 4:kv * 4 + 4, :].rearrange("p (G h) b -> p h G b", h=2)[:, hf, :, 0:nblk],
                            in_=HB[:, 0:64].rearrange("p (G b) -> p G b", G=2)[:, :, 0:nblk],
                            func=AF.Gelu_apprx_tanh, bias=hbias[:, kv:kv + 1]), [hbn, 'hbias'], ['hid'])
                chk('cmp2')
                b0 = 1 if t0 == 0 else 0
                cfirst = t0 // 16 - 1 + b0
                nst = nblk - b0
                for G_ in range(2):
                    for hf in range(2):
                        g = 2 * G_ + hf
                        PE(lambda e, g=g, hf=hf: e.matmul(G[0][64 * hf:64 * hf + 64, 0:nblk], lhsT=w2k[:, :], rhs=hid[:, g, 0:nblk],
                                                          start=True, stop=True), ['w2k', 'hid'], ['G0'])
                    st_, stn = next_stg()
                    headnorm(G[0], 'G0', nblk, 1, st_[:, 0:nblk], stn)
                    STORE(KCT[:, G_, cfirst:cfirst + nst], st_[:, b0:nblk], [stn], ['KCT'], stn)
                chk('cmp3')
                for g in range(4):
                    PE(lambda e, g=g: e.matmul(G[0][0:nblk, g * 64:(g + 1) * 64], lhsT=hid[:, 4 + g, 0:nblk], rhs=w2v[:, :],
                                               start=True, stop=True), ['w2v', 'hid'], ['G0'])
                ACT(lambda e: e.copy(out=vcs[0:nblk, :, :], in_=G[0][0:nblk, 0:256].rearrange("p (g d) -> p g d", g=4)), ['G0'], ['vcs'])
                if b0 == 0:
                    STORE(VCM[cfirst:cfirst + nst, :, :], vcs[0:nblk, :, :], ['vcs'], ['VCM'], 'vcs')
                else:
                    STORE(VCM[cfirst:cfirst + nst, :, :], vcs[1:nblk, :, :], ['vcs'], ['VCM'], 'vcs')
                chk('cmp')
                wks, wksn = ws.next(('win', C_KS))
                for idx in range(4):
                    pa, pan = nextA()
                    for k in range(8):
                        PE(lambda e, k=k, pa=pa, idx=idx: e.matmul(pa[:, :W], lhsT=wks[:, k, idx * 128:(idx + 1) * 128], rhs=hT[:, k, :W],
                                                                   start=(k == 0), stop=(k == 7)), [wksn, 'hT'], [pan])
                    st_, stn = next_stg()
                    headnorm(pa, pan, W, 2 if idx < 2 else 3, st_[:, :W], stn)
                    dst = KST if idx < 2 else KWT
                    STORE(dst[:, idx % 2, t0:t0 + W], st_[:, :W], [stn], ['KST' if idx < 2 else 'KWT'], stn)
                chk('ks')
                wvs, wvsn = ws.next(('win', C_VS))
                for s in range(nsub):
                    pa, pan = nextA()
                    for k in range(8):
                        PE(lambda e, k=k, pa=pa, s=s: e.matmul(pa[:, 0:512], lhsT=hT[:, k, s * 128:(s + 1) * 128], rhs=wvs[:, k, :],
                                                               start=(k == 0), stop=(k == 7)), [wvsn, 'hT'], [pan])
                    vs_, vsn = vstg[s % 2], 'vstg%d' % (s % 2)
                    ACT(lambda e, pa=pa, vs_=vs_: e.copy(out=vs_[:, :, :, 0:64], in_=pa[:, 0:512].rearrange("p (a g d) -> p a g d", a=2, g=4)),
                        [pan], [vsn])
                    if t0 < QS:
                        DVE(lambda e, vs_=vs_: e.memset(vs_[:, :, :, 64:65], 1.0), [], [vsn])
                        DVE(lambda e, vs_=vs_: e.tensor_scalar(out=vs_[:], in0=vs_[:], scalar1=flags[:, 0:1], scalar2=None, op0=ALU.mult),
                            [vsn, 'flags'], [vsn])
                    elif t0 == QS and s < 2:
                        DVE(lambda e, vs_=vs_: e.memset(vs_[:, :, :, 64:65], 1.0), [], [vsn])
                    r0 = t0 + s * 128
                    STORE(VS[r0:r0 + 128, :, :], vs_[:, 0, :, :], [vsn], ['VS'], vsn + 'a')
                    STORE(VW[r0:r0 + 128, :, :], vs_[:, 1, :, :], [vsn], ['VW'], vsn + 'b')
                chk('vs')
                if not qside:
                    continue
                for half in range(2):
                    wq, wqn = ws.next(('win', C_Q + 512 * half))
                    for mloc in range(4):
                        m = 4 * half + mloc
                        pa, pan = nextA()
                        for k in range(8):
                            PE(lambda e, k=k, pa=pa, mloc=mloc, wq=wq: e.matmul(pa[:, :W], lhsT=wq[:, k, mloc * 128:(mloc + 1) * 128], rhs=hT[:, k, :W],
                                                                                start=(k == 0), stop=(k == 7)), [wqn, 'hT'], [pan])
                        st_, stn = next_stg()
                        headnorm(pa, pan, W, 0, st_[:, :W], stn)
                        STORE(QT[:, m, tq0:tq0 + W], st_[:, :W], [stn], ['QT'], stn)
                wng, wngn = ws.next(('win', C_NG))
                for s in range(nsub):
                    pa, pan = nextA()
                    for k in range(8):
                        PE(lambda e, k=k, pa=pa, s=s: e.matmul(pa[:, 0:48], lhsT=hT[:, k, s * 128:(s + 1) * 128], rhs=wng[:, k, :],
                                                               start=(k == 0), stop=(k == 7)), [wngn, 'hT'], [pan])
                    gs_, gsn = gstg[s % 2], 'gstg%d' % (s % 2)
                    ACT(lambda e, pa=pa, gs_=gs_: e.activation(out=gs_[:], in_=pa[:, 0:48], func=AF.Sigmoid), [pan], [gsn])
                    STORE(GATES[tq0 + s * 128:tq0 + (s + 1) * 128, :], gs_[:], [gsn], ['GATES'], gsn)
                for jj in range(4):
                    wmr, wmrn = ws.next(('win', C_MR + 256 * jj))
                    wro, wron = ws.next(('wro', 256 * jj))
                    for jl in range(2):
                        j = 2 * jj + jl
                        pb, pbn = nextA()
                        for k in range(8):
                            PE(lambda e, k=k, pb=pb, jl=jl, wmr=wmr: e.matmul(pb[:, :W], lhsT=wmr[:, k, jl * 128:(jl + 1) * 128], rhs=hT[:, k, :W],
                                                                             start=(k == 0), stop=(k == 7)), [wmrn, 'hT'], [pbn])
                        ACT(lambda e, pb=pb: e.activation(out=sgb[:, :W], in_=pb[:, :W], func=AF.Sigmoid), [pbn], ['sgb'])
                        pa, pan = nextA()
                        for c in range(NB):
                            PE(lambda e, c=c, pa=pa, jl=jl, wro=wro: e.matmul(pa[:, :W], lhsT=wro[:, c, jl * 128:(jl + 1) * 128], rhs=hgT[:, c, :W],
                                                                             start=(c == 0), stop=(c == NB - 1)), [wron, 'hgT'], [pan])
                        st_, stn = next_stg()
                        DVE(lambda e, pa=pa, st_=st_: e.tensor_tensor(out=st_[:, :W], in0=pa[:, :W], in1=sgb[:, :W], op=ALU.mult), [pan, 'sgb'], [stn])
                        STORE(T1T[:, j, tq0:tq0 + W], st_[:, :W], [stn], ['T1T'], stn)
                for half in range(2):
                    wma, wman = ws.next(('win', C_MA + 512 * half))
                    for jl in range(4):
                        j = 4 * half + jl
                        pb, pbn = nextA()
                        for k in range(8):
                            PE(lambda e, k=k, pb=pb, jl=jl, wma=wma: e.matmul(pb[:, :W], lhsT=wma[:, k, jl * 128:(jl + 1) * 128], rhs=hT[:, k, :W],
                                                                             start=(k == 0), stop=(k == 7)), [wman, 'hT'], [pbn])
                        st_, stn = next_stg()
                        ACT(lambda e, pb=pb, st_=st_: e.activation(out=st_[:, :W], in_=pb[:, :W], func=AF.Sigmoid), [pbn], [stn])
                        STORE(SGAT[:, j, tq0:tq0 + W], st_[:, :W], [stn], ['SGAT'], stn)
            S.barrier()

        if upto == 1:
            S.emit()
            return nc
        with ExitStack() as es:
            def sb(name, shape, dt=F32):
                return es.enter_context(nc.sbuf_tensor('s2_' + name, list(shape), dt))

            def ps(name, dt=F32):
                return es.enter_context(nc.psum_tensor('p1_' + name, [128, 512 if dt == F32 else 1024], dt))
            ksT = sb('ksT', [128, 2, L], BF16)
            vsa = sb('vsa', [128, NCH, 4, 65], BF16)
            kcmp = sb('kcmp', [128, 2, NCC * 128], BF16)
            vcm = sb('vcm', [128, NCC, 4, 65], BF16)
            ovl = sb('ovl', [128, NCC, NS], BF16)
            validc = sb('validc', [128, NCC])
            cmask = sb('cmask', [128, 17, 128], BF16)
            tri = sb('tri', [128, 2, 128], BF16)
            wexp = sb('wexp', [NS, L], BF16)
            ident = sb('ident2', [128, 128], BF16)
            DVE(lambda e: e.memset(kcmp[:], 0.0), [], ['kcmp'])
            DVE(lambda e: e.memset(vcm[:], 0.0), [], ['vcm'])
            nkc = cfg['NCB']
            LOAD(kcmp[:, :, 0:nkc], KCT[:, :, 0:nkc], [], ['kcmp'], 'c2_kcmp')
            for cc in range(NCC):
                nr = min(128, nkc - cc * 128)
                LOAD(vcm[0:nr, cc, :, 0:64], VCM[cc * 128:cc * 128 + nr, :, :], [], ['vcm'], 'c2_vcm%d' % cc)
            LOAD(validc[:], validc_d[:, :], [], ['validc'], 'c2_validc')
            for cc in range(NCC):
                DVE(lambda e, cc=cc: e.memset(vcm[:, cc, :, 64:65], 1.0), ['vcm'], ['vcm'])
                DVE(lambda e, cc=cc: e.tensor_scalar(out=vcm[:, cc, :, :], in0=vcm[:, cc, :, :], scalar1=validc[:, cc:cc + 1], scalar2=None,
                                                     op0=ALU.mult), ['vcm', 'validc'], ['vcm'])
            LOAD(ovl[:], ovl_d[:, :, :], [], ['ovl'], 'c2_ovl')
            LOAD(cmask[:], cmask_d[:, :, :], [], ['cmask'], 'c2_cmask')
            LOAD(tri[:], tri_d[:, :, :], [], ['tri'], 'c2_tri')
            LOAD(wexp[:], wexp_d[:, :], [], ['wexp'], 'c2_wexp')
            LOAD(ident[:], ident_d[:, :], [], ['ident2'], 'c2_ident')
            nld = 8
            for a in range(nld):
                c0, c1 = a * L // nld, (a + 1) * L // nld
                LOAD(ksT[:, :, c0:c1], KST[:, :, c0:c1], [], ['ksT'], 'c2_ks%d' % a)
                j0, j1 = a * NCH // nld, (a + 1) * NCH // nld
                LOAD(vsa[:, j0:j1, :, :], VS[j0 * 128:j1 * 128, :, :].rearrange("(j p) g d -> p j g d", p=128), [], ['vsa'], 'c2_vs%d' % a)

            Sb = [ps('S%d' % i) for i in range(2)]
            Mbs = [ps('Mb0'), ps('Mb1')]
            OS = ps('OS')
            OCM = ps('OC')
            OW = OCM
            IMP = ps('IMP')
            Tb = ps('Tb2', BF16)
            qt = [sb('qt%d' % i, [128, 8, 128], BF16) for i in range(2)]
            gt = [sb('gt%d' % i, [128, 48]) for i in range(2)]
            rkc = [sb('rkc%d' % i, [128, 2 * NS]) for i in range(2)]
            kwt = [sb('kwt%d' % i, [128, 2, 640], BF16) for i in range(2)]
            vwt = [sb('vwt%d' % i, [128, 5, 4, 65], BF16) for i in range(2)]
            Eb = [sb('E%d' % i, [128, 512], BF16) for i in range(3)]
            Pb = [sb('P%d' % i, [128, 512], BF16) for i in range(3)]
            otok = [sb('otok%d' % i, [128, D]) for i in range(2)]
            obf = sb('obf', [128, D], BF16)
            oTs = [sb('oTs%d' % i, [128, 8, 128], BF16) for i in range(2)]
            rz = sb('rz', [128, 4])
            wg = sb('wg', [128, 4])
            tmpo = sb('tmpo', [128, 256])
            acc = sb('acc', [128, NS])
            rank = sb('rank', [128, NS])
            rank2 = sb('rank2', [128, NS])
            m8a = sb('m8a', [128, 8])
            m8b = sb('m8b', [128, 8])
            sel = sb('sel', [128, NS], BF16)
            selT = sb('selT', [NS, 128], BF16)
            ectr = [0]

            def load_q(il):
                p = il % 2
                tq = il * 128
                t0 = QS - 128 + tq
                LOAD(qt[p][:], QT[:, :, tq:tq + 128], [], ['qt%d' % p], 'qt%d' % p)
                LOAD(gt[p][:], GATES[tq:tq + 128, :], [], ['gt%d' % p], 'gt%d' % p)
                LOAD(rkc[p][:], rankc_d[il, :, :], [], ['rkc%d' % p], 'rkc%d' % p)
                LOAD(kwt[p][:], KWT[:, :, t0 - 512:t0 + 128], [], ['kwt%d' % p], 'kwt%d' % p)
                LOAD(vwt[p][:], VW[t0 - 512:t0 + 128, :, :].rearrange("(j p) g d -> p j g d", p=128), [], ['vwt%d' % p], 'vwt%d' % p)

            def chunk_pipeline(items, il, g, OB, obn):
                p = il % 2
                G_, hf = g // 2, g % 2
                qrhs = qt[p][64 * hf:64 * hf + 64, 4 * G_:4 * G_ + 4, :]
                n = len(items)
                st = {}

                def front(idx):
                    it = items[idx]
                    sbk = idx % 2
                    PE(lambda e: e.matmul(Sb[sbk][:, :].rearrange("p (r q) -> p r q", r=4), lhsT=it['lhsT'], rhs=qrhs, start=True, stop=True),
                       it['kreads'] + ['qt%d' % p], ['S%d' % sbk])
                    if it.get('selmask') is not None:
                        j = it['selmask']
                        PE(lambda e: e.matmul(Mbs[sbk][:, 0:128], lhsT=wexp[:, j * 128:(j + 1) * 128], rhs=selT[:, :],
                                              start=True, stop=True), ['wexp', 'selT'], ['Mb%d' % sbk])
                    ei = ectr[0] % 3
                    ectr[0] += 1
                    ACT(lambda e: e.activation(out=Eb[ei][:], in_=Sb[sbk][:, :], func=AF.Exp, scale=0.125), ['S%d' % sbk], ['E%d' % ei])
                    src, srcn = Eb[ei], 'E%d' % ei
                    if it.get('selmask') is not None:
                        DVE(lambda e: e.tensor_tensor(out=Pb[ei][:].rearrange("p (r q) -> p r q", r=4), in0=Eb[ei][:].rearrange("p (r q) -> p r q", r=4),
                                                      in1=Mbs[sbk][:, 0:128].unsqueeze(1).broadcast_to([128, 4, 128]), op=ALU.mult),
                            ['E%d' % ei, 'Mb%d' % sbk], ['P%d' % ei])
                        src, srcn = Pb[ei], 'P%d' % ei
                    elif it.get('mask') is not None:
                        mk, mkn = it['mask']
                        DVE(lambda e: e.tensor_tensor(out=Pb[ei][:].rearrange("p (r q) -> p r q", r=4), in0=Eb[ei][:].rearrange("p (r q) -> p r q", r=4),
                                                      in1=mk.unsqueeze(1).broadcast_to([128, 4, 128]), op=ALU.mult),
                            ['E%d' % ei, mkn], ['P%d' % ei])
                        src, srcn = Pb[ei], 'P%d' % ei
                    st[idx] = (src, srcn)

                def back(idx):
                    it = items[idx]
                    src, srcn = st[idx]
                    for r in range(4):
                        for (ob, obname, rhs, rreads, width) in it['outs']:
                            PE(lambda e, r=r, ob=ob, rhs=rhs, width=width, first=(idx == 0 and r == 0): e.matmul(
                                ob[:, r * width:(r + 1) * width], lhsT=src[:, r * 128:(r + 1) * 128], rhs=rhs,
                                start=first, stop=(idx == n - 1), skip_group_check=True), [srcn] + rreads, [obname])
                front(0)
                for idx in range(n):
                    if idx + 1 < n:
                        front(idx + 1)
                    back(idx)

            def finish_branch(il, g, OB, obn, e_idx, first):
                p = il % 2
                ot, otn = otok[p], 'otok%d' % p
                ob3 = OB[:, 0:260].rearrange("p (r d) -> p r d", r=4)
                g3 = gt[p][:, :].rearrange("p (h e) -> p h e", e=3)
                DVE(lambda e: e.tensor_scalar(out=rz[:], in0=ob3[:, :, 64], scalar1=1e-30, scalar2=None, op0=ALU.max), [obn], ['rz'])
                DVE(lambda e: e.reciprocal(out=rz[:], in_=rz[:]), ['rz'], ['rz'])
                DVE(lambda e: e.tensor_tensor(out=wg[:], in0=rz[:], in1=g3[:, 4 * g:4 * g + 4, e_idx], op=ALU.mult), ['rz', 'gt%d' % p], ['wg'])
                dst = ot[:, 256 * g:256 * (g + 1)].rearrange("p (r d) -> p r d", r=4)
                if first:
                    DVE(lambda e: e.tensor_tensor(out=dst, in0=ob3[:, :, 0:64], in1=wg[:, :].unsqueeze(2).broadcast_to([128, 4, 64]), op=ALU.mult),
                        [obn, 'wg'], [otn])
                else:
                    DVE(lambda e: e.tensor_tensor(out=tmpo[:].rearrange("p (r d) -> p r d", r=4), in0=ob3[:, :, 0:64],
                                                  in1=wg[:, :].unsqueeze(2).broadcast_to([128, 4, 64]), op=ALU.mult), [obn, 'wg'], ['tmpo'])
                    DVE(lambda e: e.tensor_tensor(out=ot[:, 256 * g:256 * (g + 1)], in0=ot[:, 256 * g:256 * (g + 1)], in1=tmpo[:], op=ALU.add),
                        [otn, 'tmpo'], [otn])

            def warm(p, n):
                for k in range(n):
                    PE(lambda e: e.matmul(Sb[0][:, :].rearrange("p (r q) -> p r q", r=4), lhsT=ident[:], rhs=qt[p][:, 0:4, :],
                                          start=True, stop=True), ['ident2', 'qt%d' % p], ['S0'])

            load_q(0)
            for il in range(NQT):
                if il > 0 and il % 6 == 0:
                    S.barrier()
                if il + 1 < NQT:
                    load_q(il + 1)
                p = il % 2
                i = (QS - 128) // 128 + il
                if il % 6 == 0:
                    warm(p, 40)
                for g in range(4):
                    G_, hf = g // 2, g % 2
                    ncc = i // 16 + 1
                    items = []
                    for cc in range(ncc):
                        mask = None
                        if cc == i // 16:
                            mask = (cmask[:, i % 16, :], 'cmask')
                        elif cc == i // 16 - 1 and i % 16 == 0:
                            mask = (cmask[:, 16, :], 'cmask')
                        items.append(dict(lhsT=kcmp[64 * hf:64 * hf + 64, G_, cc * 128:(cc + 1) * 128], kreads=['kcmp'], mask=mask,
                                          outs=[(OCM, 'OC', vcm[:, cc, g, :], ['vcm'], 65), (IMP, 'IMP', ovl[:, cc, :], ['ovl'], NS)]))
                    chunk_pipeline(items, il, g, OCM, 'OC')
                    warm(p, 24)
                    finish_branch(il, g, OCM, 'OC', 0, True)
                    for r in range(4):
                        DVE(lambda e, r=r: e.scalar_tensor_tensor(out=acc[:], in0=IMP[:, r * NS:(r + 1) * NS], scalar=rz[:, r:r + 1],
                                                                  in1=(rkc[p][:, 0:NS] if r == 0 else acc[:]), op0=ALU.mult, op1=ALU.add),
                            ['IMP', 'rz', 'rkc%d' % p, 'acc'], ['acc'])
                    DVE(lambda e: e.tensor_tensor(out=rank[:], in0=acc[:], in1=rkc[p][:, NS:2 * NS], op=ALU.mult), ['acc', 'rkc%d' % p], ['rank'])
                    DVE(lambda e: e.max(out=m8a[:], in_=rank[:]), ['rank'], ['m8a'])
                    DVE(lambda e: e.match_replace(out=rank2[:], in_to_replace=m8a[:], in_values=rank[:], imm_value=-1.0), ['rank', 'm8a'], ['rank2'])
                    DVE(lambda e: e.max(out=m8b[:], in_=rank2[:]), ['rank2'], ['m8b'])
                    DVE(lambda e: e.tensor_scalar(out=sel[:], in0=rank[:], scalar1=m8b[:, 7:8], scalar2=None, op0=ALU.is_ge), ['rank', 'm8b'], ['sel'])
                    PE(lambda e: e.transpose(out=Tb[0:NS, 0:128], in_=sel[:, :], identity=ident[:]), ['sel', 'ident2'], ['Tb2'])
                    ACT(lambda e: e.copy(out=selT[:, :], in_=Tb[0:NS, 0:128]), ['Tb2'], ['selT'])
                    items = []
                    for j in range(i + 1):
                        it = dict(lhsT=ksT[64 * hf:64 * hf + 64, G_, j * 128:(j + 1) * 128], kreads=['ksT'],
                                  outs=[(OS, 'OS', vsa[:, j, g, :], ['vsa'], 65)])
                        if j < i:
                            it['selmask'] = j
                        else:
                            it['mask'] = (tri[:, 0, :], 'tri')
                        items.append(it)
                    chunk_pipeline(items, il, g, OS, 'OS')
                    finish_branch(il, g, OS, 'OS', 1, False)
                    items = []
                    for jl in range(5):
                        mask = None
                        if jl == 4:
                            mask = (tri[:, 0, :], 'tri')
                        elif jl == 0:
                            mask = (tri[:, 1, :], 'tri')
                        items.append(dict(lhsT=kwt[p][64 * hf:64 * hf + 64, G_, jl * 128:(jl + 1) * 128], kreads=['kwt%d' % p], mask=mask,
                                          outs=[(OW, 'OC', vwt[p][:, jl, g, :], ['vwt%d' % p], 65)]))
                    chunk_pipeline(items, il, g, OW, 'OC')
                    finish_branch(il, g, OW, 'OC', 2, False)
                ACT(lambda e, p=p: e.copy(out=obf[:], in_=otok[p][:]), ['otok%d' % p], ['obf'])
                for m in range(8):
                    PE(lambda e, m=m: e.transpose(out=Tb[:, m * 128:(m + 1) * 128], in_=obf[:, m * 128:(m + 1) * 128], identity=ident[:]),
                       ['obf', 'ident2'], ['Tb2'])
                ACT(lambda e, p=p: e.copy(out=oTs[p][:], in_=Tb[:, :].rearrange("p (m q) -> p m q", m=8)), ['Tb2'], ['oTs%d' % p])
                tq = il * 128
                STORE(OT[:, 0:4, tq:tq + 128], oTs[p][:, 0:4, :], ['oTs%d' % p], ['OT'], 'oTs%da' % p)
                STORE(OT[:, 4:8, tq:tq + 128], oTs[p][:, 4:8, :], ['oTs%d' % p], ['OT'], 'oTs%db' % p)
            S.barrier()

        if upto == 2:
            S.emit()
            return nc
        with ExitStack() as es:
            def sb(name, shape, dt=F32):
                return es.enter_context(nc.sbuf_tensor('s3_' + name, list(shape), dt))

            def ps(name, dt=F32):
                return es.enter_context(nc.psum_tensor('p2_' + name, [128, 512 if dt == F32 else 1024], dt))
            ptiles = [(QS - 128, 128, 'H')] + [(t, 512, 'Q') for t in range(QS, L, 512)]
            plan = []
            for (t0, W, mode) in ptiles:
                plan += [('wao', 0, 512), ('wao', 512, 512), ('wout', 0, 512), ('wout', 512, 512)]
                for jj in range(6):
                    plan += [('wup', 512 * jj, 512), ('wup', D_FF + 512 * jj, 512)]
                if mode == 'Q':
                    for c in range(2):
                        for jr in range(3):
                            plan.append(('wdn', c * 512, 512, jr))
            ident = sb('ident3', [128, 128], BF16)
            ffnc = sb('ffnc', [128, 48, 4])
            LOAD(ident[:], ident_d[:, :], [], ['ident3'], 'c3_ident')
            LOAD(ffnc[:], ffnc_d[:, :, :], [], ['ffnc'], 'c3_ffnc')
            slots = [sb('w3s%d' % i, [128, 4096], BF16) for i in range(4)]
            wst = dict(issued=0, taken=0)

            def w_issue():
                i = wst['issued']
                pc = plan[i]
                sl = i % 4
                view = slots[sl][:, :].rearrange("p (k n) -> p k n", k=8)
                if pc[0] == 'wdn':
                    src = wscr['wdn'][:, 8 * pc[3]:8 * pc[3] + 8, pc[1]:pc[1] + 512]
                else:
                    src = wscr[pc[0]][:, :, pc[1]:pc[1] + 512]
                LOAD(view, src, [], ['w3s%d' % sl], 'w3s%d' % sl)
                wst['issued'] += 1

            def w_next(expect):
                while wst['issued'] < min(len(plan), wst['taken'] + 3):
                    w_issue()
                i = wst['taken']
                assert plan[i][:2] == expect, (plan[i], expect)
                wst['taken'] += 1
                sl = i % 4
                return slots[sl][:, :].rearrange("p (k n) -> p k n", k=8), 'w3s%d' % sl

            A = [ps('A3_%d' % i) for i in range(3)]
            Dn = [ps('Dn%d' % i) for i in range(4)]
            Tb = ps('Tb3', BF16)
            acc_i = [0]

            def nextA():
                i = acc_i[0] % 3
                acc_i[0] += 1
                return A[i], 'A3_%d' % i
            xt = sb('x3', [128, 4, D])
            oT = sb('oT3', [128, 8, 512], BF16)
            t1 = sb('t13', [128, 8, 512], BF16)
            sga = sb('sga3', [128, 8, 512], BF16)
            mT = sb('mT3', [128, 8, 512], BF16)
            h2T = sb('h2T', [128, 8, 512], BF16)
            actT = sb('actT', [128, 24, 512], BF16)
            junk = sb('junk3', [128, D], BF16)
            xn = sb('xn3', [128, D], BF16)
            ssq = sb('ssq3', [128, 8])
            tmpm = sb('tmpm', [128, 512])
            rawb = [sb('rawb%d' % i, [128, 514]) for i in range(2)]
            cv = [sb('cv%d' % i, [128, 512]) for i in range(2)]
            gl = sb('gl', [128, 512])
            fhalo = sb('fhalo', [128, 48, 2])
            DVE(lambda e: e.memset(fhalo[:], 0.0), [], ['fhalo'])

            for ti, (t0, W, mode) in enumerate(ptiles):
                nsub = W // 128
                tq0 = t0 - (QS - 128)
                LOAD(xt[:, 0:nsub, :], xloc[t0:t0 + W, :].rearrange("(s p) d -> p s d", p=128), [], ['x3'], 'x3')
                LOAD(oT[:, :, :W], OT[:, :, tq0:tq0 + W], [], ['oT3'], 'oT3')
                LOAD(t1[:, :, :W], T1T[:, :, tq0:tq0 + W], [], ['t13'], 't13')
                LOAD(sga[:, :, :W], SGAT[:, :, tq0:tq0 + W], [], ['sga3'], 'sga3')
                for half in range(2):
                    w_, wn = w_next(('wao', 512 * half))
                    for jl in range(4):
                        j = 4 * half + jl
                        pa, pan = nextA()
                        for m in range(8):
                            PE(lambda e, m=m, pa=pa, jl=jl, w_=w_: e.matmul(pa[:, :W], lhsT=w_[:, m, jl * 128:(jl + 1) * 128], rhs=oT[:, m, :W],
                                                                           start=(m == 0), stop=(m == 7)), [wn, 'oT3'], [pan])
                        DVE(lambda e, pa=pa, j=j: e.tensor_tensor(out=tmpm[:, :W], in0=pa[:, :W], in1=sga[:, j, :W], op=ALU.mult), [pan, 'sga3'], ['tmpm'])
                        DVE(lambda e, j=j: e.tensor_tensor(out=mT[:, j, :W], in0=tmpm[:, :W], in1=t1[:, j, :W], op=ALU.add), ['tmpm', 't13'], ['mT3'])
                for c in range(2):
                    w_, wn = w_next(('wout', 512 * c))
                    for s in range(nsub):
                        pa, pan = nextA()
                        for j in range(8):
                            PE(lambda e, j=j, pa=pa, s=s, w_=w_: e.matmul(pa[:, 0:512], lhsT=mT[:, j, s * 128:(s + 1) * 128], rhs=w_[:, j, :],
                                                                         start=(j == 0), stop=(j == 7)), [wn, 'mT3'], [pan])
                        DVE(lambda e, pa=pa, s=s, c=c: e.tensor_tensor(out=xt[:, s, c * 512:(c + 1) * 512], in0=pa[:, 0:512],
                                                                       in1=xt[:, s, c * 512:(c + 1) * 512], op=ALU.add), [pan, 'x3'], ['x3'])
                for s in range(nsub):
                    ACT(lambda e, s=s: e.activation(out=junk[:], in_=xt[:, s, :], func=AF.Square, accum_out=ssq[:, s:s + 1]), ['x3'], ['junk3', 'ssq3'])
                ACT(lambda e: e.activation(out=ssq[:, 4:4 + nsub], in_=ssq[:, 0:nsub], func=AF.Sqrt, scale=1.0 / D, bias=EPS), ['ssq3'], ['ssq3'])
                DVE(lambda e: e.reciprocal(out=ssq[:, 4:4 + nsub], in_=ssq[:, 4:4 + nsub]), ['ssq3'], ['ssq3'])
                for s in range(nsub):
                    ACT(lambda e, s=s: e.activation(out=xn[:], in_=xt[:, s, :], func=AF.Copy, scale=ssq[:, 4 + s:5 + s]), ['x3', 'ssq3'], ['xn3'])
                    for k in range(8):
                        PE(lambda e, k=k: e.transpose(out=Tb[:, k * 128:(k + 1) * 128], in_=xn[:, k * 128:(k + 1) * 128], identity=ident[:]),
                           ['xn3', 'ident3'], ['Tb3'])
                    DVE(lambda e, s=s: e.tensor_copy(out=h2T[:, :, s * 128:(s + 1) * 128], in_=Tb[:, :].rearrange("p (k q) -> p k q", k=8)),
                        ['Tb3'], ['h2T'])
                for jj in range(6):
                    wg_, wgn = w_next(('wup', 512 * jj))
                    wv_, wvn = w_next(('wup', D_FF + 512 * jj))
                    for jl in range(4):
                        j = 4 * jj + jl
                        cvs = []
                        for which, (w_, wn) in enumerate(((wg_, wgn), (wv_, wvn))):
                            cidx = j + 24 * which
                            pa, pan = nextA()
                            for k in range(8):
                                PE(lambda e, k=k, pa=pa, jl=jl, w_=w_: e.matmul(pa[:, :W], lhsT=w_[:, k, jl * 128:(jl + 1) * 128], rhs=h2T[:, k, :W],
                                                                               start=(k == 0), stop=(k == 7)), [wn, 'h2T'], [pan])
                            rb, rbn = rawb[which], 'rawb%d' % which
                            DVE(lambda e, rb=rb, cidx=cidx: e.tensor_copy(out=rb[:, 0:2], in_=fhalo[:, cidx, :]), ['fhalo'], [rbn])
                            ACT(lambda e, rb=rb, pa=pa: e.copy(out=rb[:, 2:2 + W], in_=pa[:, :W]), [pan], [rbn])
                            ACT(lambda e, rb=rb, cidx=cidx: e.copy(out=fhalo[:, cidx, :], in_=rb[:, W:W + 2]), [rbn], ['fhalo'])
                            if mode == 'H':
                                continue
                            cc_, ccn = cv[which], 'cv%d' % which
                            ACT(lambda e, pa=pa, cc_=cc_, cidx=cidx: e.activation(out=cc_[:, :W], in_=pa[:, :W], func=AF.Identity,
                                                                                scale=ffnc[:, cidx, 2:3], bias=ffnc[:, cidx, 3:4]), [pan, 'ffnc'], [ccn])
                            for k in range(2):
                                DVE(lambda e, k=k, rb=rb, cc_=cc_, cidx=cidx: e.scalar_tensor_tensor(
                                    out=cc_[:, :W], in0=rb[:, k:k + W], scalar=ffnc[:, cidx, k:k + 1], in1=cc_[:, :W],
                                    op0=ALU.mult, op1=ALU.add), [rbn, ccn, 'ffnc'], [ccn])
                        if mode == 'H':
                            continue
                        ACT(lambda e: e.activation(out=gl[:, :W], in_=cv[0][:, :W], func=AF.Gelu_apprx_tanh), ['cv0'], ['gl'])
                        DVE(lambda e, j=j: e.tensor_tensor(out=actT[:, j, :W], in0=gl[:, :W], in1=cv[1][:, :W], op=ALU.mult), ['gl', 'cv1'], ['actT'])
                if mode == 'H':
                    continue
                for c in range(2):
                    for jr in range(3):
                        w_, wn = w_next(('wdn', 512 * c))
                        for s in range(nsub):
                            for k in range(8):
                                j = 8 * jr + k
                                PE(lambda e, j=j, k=k, s=s, w_=w_, jr=jr: e.matmul(Dn[s][:, 0:512], lhsT=actT[:, j, s * 128:(s + 1) * 128], rhs=w_[:, k, :],
                                                                                 start=(jr == 0 and k == 0), stop=(jr == 2 and k == 7)),
                                   [wn, 'actT'], ['Dn%d' % s])
                    for s in range(nsub):
                        DVE(lambda e, s=s, c=c: e.tensor_tensor(out=xt[:, s, c * 512:(c + 1) * 512], in0=Dn[s][:, 0:512],
                                                                in1=xt[:, s, c * 512:(c + 1) * 512], op=ALU.add), ['Dn%d' % s, 'x3'], ['x3'])
                r0 = t0 - QS
                STORE(y[r0:r0 + W, :].rearrange("(s p) d -> p s d", p=128), xt[:, 0:nsub, :], ['x3'], ['y'], 'ystore')
            S.barrier()
        S.emit()
    return nc


def _bf(a):
    return np.ascontiguousarray(a).astype(ml_dtypes.bfloat16)


def host_consts(cfg, h):
    L, QS, NS, NCC, NQT = cfg['L'], cfg['QS'], cfg['NS'], cfg['NCC'], cfg['NQT']
    B0 = QS // 64 if h == 0 else 0
    n = np.arange(NS)[None, :]
    rankc = np.zeros((NQT, 128, 2 * NS), np.float32)
    for il in range(NQT):
        t = QS - 128 + 128 * il + np.arange(128)[:, None]
        cur = t // 64
        C = (n <= cur) & (n >= B0)
        forced = (n == B0) | (n == cur) | (n == cur - 1)
        rankc[il, :, :NS] = 16.0 + 16.0 * forced
        rankc[il, :, NS:] = C
    c = (np.arange(NCC)[None, :] * 128 + np.arange(128)[:, None])
    valid = (c <= L // 16 - 2) & ((16 * c >= QS) if h == 0 else True)
    cc = c[:, :, None]
    lo = np.maximum(cc * 16, n[None] * 64)
    hi = np.minimum(cc * 16 + 32, (n[None] + 1) * 64)
    ovl = np.maximum(hi - lo, 0) / 16.0 * valid[:, :, None]
    cl = np.arange(128)[:, None, None]
    mm = np.arange(17)[None, :, None]
    ql = np.arange(128)[None, None, :]
    cmask = (16 * cl + 31 - ql <= 128 * mm)
    kl = np.arange(128)[:, None]
    qq = np.arange(128)[None, :]
    tri = np.stack([kl <= qq, kl > qq], axis=1)
    wexp = (np.arange(L)[None, :] // 64 == np.arange(NS)[:, None])
    pp = np.arange(128)
    bones = (pp[:, None] // 64 == pp[None, :] // 64) / 64.0
    flags = np.zeros((128, 2), np.float32)
    flags[:, :] = 1.0 if h == 1 else 0.0
    return dict(rankc=rankc, ovl=_bf(ovl), validc=valid.astype(np.float32), cmask=_bf(cmask), tri=_bf(tri), wexp=_bf(wexp),
                ident=_bf(np.eye(128)), bones=_bf(bones), flags=flags)


def host_weights(inp):
    f32 = np.float32
    w_in = np.asarray(inp['w_in'], f32)
    kvw = NKV * HD
    sizes = [D_RNN, D_RNN, NH * HD] + [kvw] * 6 + [3 * NH, D, D]
    offs = np.concatenate([[0], np.cumsum(sizes)])
    sec = {nm: w_in[:, offs[i]:offs[i + 1]] for i, nm in enumerate(
        ['rx', 'rg', 'q', 'kc', 'vc', 'ks', 'vs', 'kw', 'vw', 'ng', 'mr', 'ma'])}
    qcols = []
    for G_ in range(2):
        for r in range(4):
            for hd_ in (8 * G_ + r, 8 * G_ + 4 + r):
                qcols += list(range(hd_ * 64, hd_ * 64 + 64))
    win = np.concatenate([sec['rx'], sec['rg'], sec['q'][:, qcols], sec['kc'], sec['vc'], sec['ks'], sec['kw'],
                          sec['vs'], sec['vw'], sec['ng'], sec['mr'], sec['ma']], axis=1)
    assert win.shape[1] == N_IN

    def pkn(w, kp):
        K, N = w.shape
        return np.ascontiguousarray(w.reshape(K // kp, kp, N).transpose(1, 0, 2))
    out = {}
    out['f_win'] = pkn(win, 128)
    out['f_wro'] = pkn(np.asarray(inp['w_rnn_out'], f32), BW)
    out['f_wao'] = pkn(np.asarray(inp['w_attn_out'], f32), 128)
    out['f_wout'] = pkn(np.asarray(inp['w_out'], f32), 128)
    out['f_wup'] = pkn(np.asarray(inp['w_up'], f32), 128)
    out['f_wdn'] = pkn(np.asarray(inp['w_down'], f32), 128)
    out['f_rga'] = np.ascontiguousarray(np.asarray(inp['rg_w_a'], f32).transpose(1, 0, 2))
    out['f_rgx'] = np.ascontiguousarray(np.asarray(inp['rg_w_x'], f32).transpose(1, 0, 2))
    for nm, key in (('w1k', 'cmp_w1_k'), ('w1v', 'cmp_w1_v')):
        w1 = np.asarray(inp[key], f32).reshape(32, 64, 128).transpose(1, 0, 2)
        out['f_' + nm] = np.ascontiguousarray(np.concatenate([w1, w1], axis=0))
    out['f_w2k'] = np.asarray(inp['cmp_w2_k'], f32).reshape(128, 1, 64).copy()
    out['f_w2v'] = np.asarray(inp['cmp_w2_v'], f32).reshape(128, 1, 64).copy()
    for nm, key in (('pek', 'cmp_pe_k'), ('pev', 'cmp_pe_v')):
        pe = np.asarray(inp[key], f32).T
        out['f_' + nm] = np.ascontiguousarray(np.concatenate([pe, pe], axis=0).reshape(128, 1, 32))
    out['g1'] = np.ascontiguousarray(np.asarray(inp['norm1_g'], f32).reshape(8, 128).T)
    out['g2'] = np.ascontiguousarray(np.asarray(inp['norm2_g'], f32).reshape(8, 128).T)
    rn = np.zeros((BW, NB, 8), f32)
    cw = np.asarray(inp['rnn_conv_w'], f32)
    for k in range(4):
        rn[:, :, k] = cw[k].reshape(NB, BW).T
    for k, key in ((4, 'rnn_conv_b'), (5, 'rg_b_a'), (6, 'rg_b_x'), (7, 'lru_lambda')):
        rn[:, :, k] = np.asarray(inp[key], f32).reshape(NB, BW).T
    out['rnnc'] = rn
    fc = np.zeros((128, 48, 4), f32)
    fw = np.asarray(inp['ffn_conv_w'], f32)
    for k in range(3):
        fc[:, :, k] = fw[k].reshape(48, 128).T
    fc[:, :, 3] = np.asarray(inp['ffn_conv_b'], f32).reshape(48, 128).T
    out['ffnc'] = fc
    hg = np.zeros((128, 4), f32)
    hg[:, 0] = np.tile(np.asarray(inp['q_norm_g'], f32), 2)
    hg[:, 1] = np.tile(np.asarray(inp['k_cmp_norm_g'], f32), 2)
    hg[:, 2] = np.tile(np.asarray(inp['k_slc_norm_g'], f32), 2)
    hg[:, 3] = np.tile(np.asarray(inp['k_win_norm_g'], f32), 2)
    out['hgains'] = hg
    return out


_CACHE = {}


def run_cfg(cfg, inputs, x, debug=False, upto=3):
    B = x.shape[0]
    half = cfg['L'] - cfg['QS']
    key = (cfg['L'], cfg['QS'], debug)
    if key not in _CACHE:
        _CACHE[key] = build_program(cfg, debug, upto)
    nc = _CACHE[key]
    hw = host_weights(inputs)
    consts = [host_consts(cfg, 0), host_consts(cfg, 1)]
    in_maps = []
    for b in range(B):
        for h in range(2):
            xl = np.zeros((cfg['L'], D), np.float32)
            assert cfg['QS'] == half
            if h == 0:
                xl[cfg['QS']:] = x[b, :half]
            else:
                xl[:] = x[b]
            m = dict(hw)
            m.update(consts[h])
            m['xloc'] = xl
            in_maps.append(m)
    res = run_bass_kernel_spmd(nc, in_maps, core_ids=list(range(len(in_maps))))
    out = np.zeros((B, 2 * half, D), np.float32)
    for b in range(B):
        for h in range(2):
            out[b, h * half:(h + 1) * half] = res.results[2 * b + h]['y']
    return out, res


def kernel(**inputs):
    x = np.asarray(inputs['x'], np.float32)
    cfg = make_cfg(8192, 4096)
    out, _ = run_cfg(cfg, inputs, x)
    return out
```

```python
import numpy as np
import ml_dtypes
from contextlib import ExitStack
import concourse.bass as bass
import concourse.mybir as mybir
from concourse.bass_utils import run_bass_kernel_spmd

F32 = mybir.dt.float32
BF16 = mybir.dt.bfloat16
AF = mybir.ActivationFunctionType
ALU = mybir.AluOpType

D = 1024
KD = 8
D_RNN = 1344
NB = 16
BW = 84
HD = 64
NH = 16
NKV = 4
D_FF = 3072
EPS = 1e-6
C_RX, C_RG, C_Q, C_KC, C_VC, C_KS, C_KW, C_VS, C_VW, C_NG, C_MR, C_MA = (
    0, 1344, 2688, 3712, 3968, 4224, 4480, 4736, 4992, 5248, 5296, 6320)
N_IN = 7344


import os


class _Stop(Exception):
    pass


_STOPCTX = {}


def chk(tag):
    if os.environ.get('P1STOP') == tag:
        _STOPCTX['S'].barrier()
        _STOPCTX['S'].emit()
        raise _Stop()


class _Rec:
    def __init__(self):
        self.call = None

    def __getattr__(self, name):
        def f(*a, **k):
            self.call = (name, a, k)
            return self
        return f


class Sched:
    CE = ['pe', 'act', 'dve', 'pool']

    def __init__(self, nc, es):
        self.nc = nc
        self.es = es
        self.gen = 0
        self.dfree = {'sp': [], 'pool': []}
        self.ndsem = 0
        self.sem = {e: es.enter_context(nc.semaphore("sem_" + e)) for e in self.CE}
        self.cnt = {e: 0 for e in self.CE}
        self.ops = {e: [] for e in self.CE + ['sp']}
        self.waited = {e: {} for e in self.CE + ['sp']}
        self.res = {}
        self.dsem = {}
        self.semobj = {}
        for e in self.CE:
            self.semobj[('c', e, 0)] = self.sem[e]
        self.nops = 0

    def _res(self, name):
        r = self.res.get(name)
        if r is None:
            r = {'w': None, 'r': {}}
            self.res[name] = r
        return r

    def op(self, eng, fn, reads=(), writes=(), dma=None):
        need = {}

        def add(ev, kind):
            if ev is None:
                return
            sk, val, src = ev
            if dma is None and src == eng and kind != 'raw' and eng == 'pe':
                return
            if need.get(sk, 0) < val:
                need[sk] = val
        for r in reads:
            add(self._res(r)['w'], 'raw')
        for w in writes:
            rr = self._res(w)
            add(rr['w'], 'waw')
            for ev in rr['r'].values():
                add(ev, 'war')
        if dma is not None:
            if dma not in self.dsem:
                if self.dfree[eng]:
                    s, c0 = self.dfree[eng].pop()
                else:
                    self.ndsem += 1
                    s, c0 = self.es.enter_context(self.nc.semaphore("dsem%d" % self.ndsem)), 0
                self.dsem[dma] = [s, c0, c0, eng]
                self.semobj[('d', dma)] = s
            d = self.dsem[dma]
            if d[1] > d[2]:
                add((('d', dma), d[1], 'dmaq'), 'raw')
        waits = []
        wd = self.waited[eng]
        for sk, val in need.items():
            if wd.get(sk, 0) >= val:
                continue
            wd[sk] = val
            waits.append((self.semobj[sk], val))
        if dma is not None:
            d = self.dsem[dma]
            d[1] += 16
            ev = (('d', dma), d[1], 'dmaq')
            inc = (d[0], 16)
        else:
            self.cnt[eng] += 1
            ev = (('c', eng, self.gen), self.cnt[eng], eng)
            inc = (self.sem[eng], 1)
        rec = _Rec()
        fn(rec)
        assert rec.call is not None
        self.ops[eng].append((waits, rec.call, inc))
        self.nops += 1
        for w in writes:
            self.res[w] = {'w': ev, 'r': {}}
        for r in reads:
            self._res(r)['r'][ev[0]] = ev
        return ev

    def barrier(self, fresh=True):
        evs = [(('c', e, self.gen), self.cnt[e]) for e in self.CE if self.cnt[e] > 0]
        evs += [(('d', k), d[1]) for k, d in self.dsem.items() if d[1] > d[2]]
        for e in self.CE + ['sp']:
            waits = []
            wd = self.waited[e]
            for sk, val in evs:
                if sk == ('c', e, self.gen):
                    continue
                if wd.get(sk, 0) >= val:
                    continue
                wd[sk] = val
                waits.append((self.semobj[sk], val))
            if waits:
                self.ops[e].append((waits, None, None))
        self.res = {}
        for k, d in self.dsem.items():
            self.dfree[d[3]].append((d[0], d[1]))
        self.dsem = {}
        for e in self.waited:
            self.waited[e] = {k: v for k, v in self.waited[e].items() if k[0] != 'd'}
        if not fresh:
            return
        self.gen += 1
        for e in self.CE:
            self.sem[e] = self.es.enter_context(self.nc.semaphore("sem_%s_g%d" % (e, self.gen)))
            self.cnt[e] = 0
            self.semobj[('c', e, self.gen)] = self.sem[e]

    def emit(self):
        block = self.es.enter_context(self.nc.Block())

        def run(engobj, lst):
            for waits, fn, inc in lst:
                for s, v in waits:
                    engobj.wait_ge(s, v)
                if fn is None:
                    continue
                name, a, k = fn
                ins = getattr(engobj, name)(*a, **k)
                ins.then_inc(inc[0], inc[1])

        @block.tensor
        def _(e):
            run(e, self.ops['pe'])

        @block.scalar
        def _(e):
            run(e, self.ops['act'])

        @block.vector
        def _(e):
            run(e, self.ops['dve'])

        @block.gpsimd
        def _(e):
            run(e, self.ops['pool'])

        @block.sync
        def _(e):
            run(e, self.ops['sp'])


def make_cfg(L, QS):
    assert L % 512 == 0 and QS % 512 == 0 and QS - 128 >= 512
    cfg = dict(L=L, QS=QS, LQ=L - QS + 128, NS=L // 64, NCB=L // 16 - 1,
               NCC=(L // 16 - 1 + 127) // 128, NQT=(L - QS + 128) // 128)
    tiles = []
    t = 0
    while t < QS - 128:
        w = min(512, QS - 128 - t)
        tiles.append((t, w, 'K'))
        t += w
    tiles.append((QS - 128, 128, 'H'))
    t = QS
    while t < L:
        tiles.append((t, 512, 'Q'))
        t += 512
    cfg['tiles'] = tiles
    return cfg


WEIGHTS = [
    ('win', 128, 8, N_IN), ('wro', BW, NB, D), ('wao', 128, 8, D), ('wout', 128, 8, D),
    ('wup', 128, 8, 2 * D_FF), ('wdn', 128, 24, D), ('rga', BW, NB, BW), ('rgx', BW, NB, BW),
    ('w1k', 128, 32, 128), ('w1v', 128, 32, 128), ('w2k', 128, 1, 64), ('w2v', 128, 1, 64),
    ('pek', 128, 1, 32), ('pev', 128, 1, 32),
]


def build_program(cfg, debug=False, upto=3):
    try:
        return _build_program(cfg, debug, upto)
    except _Stop:
        return _STOPCTX['nc']


def _build_program(cfg, debug=False, upto=3):
    L, QS, LQ, NS, NCC, NQT = cfg['L'], cfg['QS'], cfg['LQ'], cfg['NS'], cfg['NCC'], cfg['NQT']
    NCH = L // 128
    nc = bass.Bass("TRN2", target_bir_lowering=False)
    ins = {}

    def IN(name, shape, dt=F32):
        ins[name] = nc.dram_tensor(name, list(shape), dt, kind="ExternalInput").ap()
        return ins[name]

    skind = "ExternalOutput" if debug else "Internal"

    def SCR(name, shape, dt=BF16):
        return nc.dram_tensor(name, list(shape), dt, kind=skind).ap()

    xloc = IN('xloc', [L, D])
    wsrc = {n: IN('f_' + n, [kp, kc, N]) for n, kp, kc, N in WEIGHTS}
    g1 = IN('g1', [128, 8])
    g2 = IN('g2', [128, 8])
    rnnc_d = IN('rnnc', [BW, NB, 8])
    ffnc_d = IN('ffnc', [128, 48, 4])
    hg_d = IN('hgains', [128, 4])
    flags_d = IN('flags', [128, 2])
    rankc_d = IN('rankc', [NQT, 128, 2 * NS])
    ovl_d = IN('ovl', [128, NCC, NS], BF16)
    validc_d = IN('validc', [128, NCC])
    cmask_d = IN('cmask', [128, 17, 128], BF16)
    tri_d = IN('tri', [128, 2, 128], BF16)
    wexp_d = IN('wexp', [NS, L], BF16)
    ident_d = IN('ident', [128, 128], BF16)
    bones_d = IN('bones', [128, 128], BF16)
    y = nc.dram_tensor('y', [L - QS, D], F32, kind="ExternalOutput").ap()

    wscr = {n: SCR('b_' + n, [kp, kc, N]) for n, kp, kc, N in WEIGHTS}
    KST = SCR('KST', [128, 2, L])
    KWT = SCR('KWT', [128, 2, L])
    VS = SCR('VS', [L, 4, 65])
    VW = SCR('VW', [L, 4, 65])
    KCT = SCR('KCT', [128, 2, NCC * 128])
    VCM = SCR('VCM', [NCC * 128, 4, 64])
    QT = SCR('QT', [128, 8, LQ])
    GATES = SCR('GATES', [LQ, 48], F32)
    T1T = SCR('T1T', [128, 8, LQ])
    SGAT = SCR('SGAT', [128, 8, LQ])
    OT = SCR('OT', [128, 8, LQ])

    with ExitStack() as es0:
        S = Sched(nc, es0)
        _STOPCTX['S'] = S
        _STOPCTX['nc'] = nc
        uid = [0]

        def PE(fn, r, w):
            return S.op('pe', fn, r, w)

        def ACT(fn, r, w):
            return S.op('act', fn, r, w)

        def DVE(fn, r, w):
            return S.op('dve', fn, r, w)

        def POOL(fn, r, w):
            return S.op('pool', fn, r, w)

        def LOAD(out, in_, r, w, key):
            return S.op('sp', lambda e: e.dma_start(out=out, in_=in_), r, w, dma=key)

        def STORE(out, in_, r, w, key):
            return S.op('pool', lambda e: e.dma_start(out=out, in_=in_), r, w, dma=key)

        class WStream:
            def __init__(self, es, plan, nslots=4, tag='ws'):
                self.plan = plan
                self.slots = [es.enter_context(nc.sbuf_tensor("ws_%s_slot%d" % (tag, i), [128, 4096], BF16))
                              for i in range(nslots)]
                self.tag = tag
                self.issued = 0
                self.taken = 0
                self.ns = nslots

            def _issue(self):
                i = self.issued
                name, c0, n = self.plan[i]
                _, kp, kc, N = [wt for wt in WEIGHTS if wt[0] == name][0]
                sl = i % self.ns
                view = self.slots[sl][0:kp, 0:kc * n].rearrange("p (k n) -> p k n", k=kc)
                LOAD(view, wscr[name][:, :, c0:c0 + n], ['b_' + name], ['%s%d' % (self.tag, sl)],
                     '%s%d' % (self.tag, sl))
                self.issued += 1

            def next(self, expect=None):
                while self.issued < min(len(self.plan), self.taken + self.ns - 1):
                    self._issue()
                i = self.taken
                name, c0, n = self.plan[i]
                if expect is not None:
                    assert expect == (name, c0), (expect, self.plan[i])
                _, kp, kc, N = [wt for wt in WEIGHTS if wt[0] == name][0]
                sl = i % self.ns
                self.taken += 1
                view = self.slots[sl][0:kp, 0:kc * n].rearrange("p (k n) -> p k n", k=kc)
                return view, '%s%d' % (self.tag, sl)

        with ExitStack() as es:
            def sb(name, shape, dt=F32):
                return es.enter_context(nc.sbuf_tensor('s0_' + name, list(shape), dt))
            NSL = 4
            stf = [sb('p0f%d' % i, [128, 4096]) for i in range(NSL)]
            stb = [sb('p0b%d' % i, [128, 4096], BF16) for i in range(NSL)]
            g12 = sb('g12', [128, 16])
            LOAD(g12[:, 0:8], g1[:, :], [], ['g12'], 'g12a')
            LOAD(g12[:, 8:16], g2[:, :], [], ['g12'], 'g12b')
            blocks = []
            for name, kp, kc, N in WEIGHTS:
                cb = min(N, 4096 // kc)
                c0 = 0
                while c0 < N:
                    n = min(cb, N - c0)
                    blocks.append((name, kp, kc, c0, n))
                    c0 += n

            def p0_views(blk):
                name, kp, kc, c0, n = blocks[blk]
                sl = blk % NSL
                fv = stf[sl][0:kp, 0:kc * n].rearrange("p (k n) -> p k n", k=kc)
                bv = stb[sl][0:kp, 0:kc * n].rearrange("p (k n) -> p k n", k=kc)
                return fv, bv, sl

            def p0_load(blk):
                name, kp, kc, c0, n = blocks[blk]
                fv, bv, sl = p0_views(blk)
                LOAD(fv, wsrc[name][:, :, c0:c0 + n], [], ['p0f%d' % sl], 'p0f%d' % sl)

            PF = 2
            for blk in range(min(PF, len(blocks))):
                p0_load(blk)
            for blk in range(len(blocks)):
                if blk + PF < len(blocks):
                    p0_load(blk + PF)
                name, kp, kc, c0, n = blocks[blk]
                fv, bv, sl = p0_views(blk)
                if name in ('win', 'wup'):
                    goff = 0 if name == 'win' else 8
                    for k in range(kc):
                        if k % 2 == 0:
                            DVE(lambda e, k=k, fv=fv, bv=bv, goff=goff: e.tensor_scalar(
                                out=bv[:, k, :], in0=fv[:, k, :], scalar1=g12[:, goff + k:goff + k + 1],
                                scalar2=None, op0=ALU.mult), ['p0f%d' % sl, 'g12'], ['p0b%d' % sl])
                        else:
                            ACT(lambda e, k=k, fv=fv, bv=bv, goff=goff: e.activation(
                                out=bv[:, k, :], in_=fv[:, k, :], func=AF.Copy,
                                scale=g12[:, goff + k:goff + k + 1]), ['p0f%d' % sl, 'g12'], ['p0b%d' % sl])
                else:
                    if blk % 2 == 0:
                        DVE(lambda e, fv=fv, bv=bv: e.tensor_copy(out=bv, in_=fv), ['p0f%d' % sl], ['p0b%d' % sl])
                    else:
                        ACT(lambda e, fv=fv, bv=bv: e.copy(out=bv, in_=fv), ['p0f%d' % sl], ['p0b%d' % sl])
                LOAD(wscr[name][:, :, c0:c0 + n], bv, ['p0b%d' % sl], ['b_' + name], 'p0s%d' % sl)
            S.barrier()

        if upto == 0:
            S.emit()
            return nc
        with ExitStack() as es:
            def sb(name, shape, dt=F32):
                return es.enter_context(nc.sbuf_tensor('s1_' + name, list(shape), dt))

            def ps(name, dt=F32):
                return es.enter_context(nc.psum_tensor('p0_' + name, [128, 512 if dt == F32 else 1024], dt))
            plan = []
            for (t0, W, mode) in cfg['tiles']:
                q = mode != 'K'
                for a in range(4):
                    plan.append(('win', C_RX + 336 * a, 336))
                    if q:
                        plan.append(('win', C_RG + 336 * a, 336))
                plan += [('win', C_KC, 512), ('win', C_KS, 512), ('win', C_VS, 512)]
                if q:
                    plan += [('win', C_Q, 512), ('win', C_Q + 512, 512), ('win', C_NG, 48)]
                    for jj in range(4):
                        plan += [('win', C_MR + 256 * jj, 256), ('wro', 256 * jj, 256)]
                    plan += [('win', C_MA, 512), ('win', C_MA + 512, 512)]
            ws = WStream(es, plan, 4, 'w1s')
            ident = sb('ident', [128, 128], BF16)
            bones = sb('bones', [128, 128], BF16)
            rga = sb('rga', [BW, NB, BW], BF16)
            rgx = sb('rgx', [BW, NB, BW], BF16)
            w1k = sb('w1k', [128, 32, 128], BF16)
            w1v = sb('w1v', [128, 32, 128], BF16)
            w2k = sb('w2k', [128, 64], BF16)
            w2v = sb('w2v', [128, 64], BF16)
            pek = sb('pek', [128, 32], BF16)
            pev = sb('pev', [128, 32], BF16)
            rnnc = sb('rnnc', [BW, NB, 8])
            hgn = sb('hgn', [128, 4])
            flags = sb('flags', [128, 2])
            LOAD(ident[:], ident_d[:, :], [], ['ident'], 'c_ident')
            LOAD(bones[:], bones_d[:, :], [], ['bones'], 'c_bones')
            LOAD(rga[:], wscr['rga'][:, :, :], ['b_rga'], ['rga'], 'c_rga')
            LOAD(rgx[:], wscr['rgx'][:, :, :], ['b_rgx'], ['rgx'], 'c_rgx')
            LOAD(w1k[:], wscr['w1k'][:, :, :], ['b_w1k'], ['w1k'], 'c_w1k')
            LOAD(w1v[:], wscr['w1v'][:, :, :], ['b_w1v'], ['w1v'], 'c_w1v')
            LOAD(w2k[:], wscr['w2k'][:, 0, :], ['b_w2k'], ['w2k'], 'c_w2k')
            LOAD(w2v[:], wscr['w2v'][:, 0, :], ['b_w2v'], ['w2v'], 'c_w2v')
            LOAD(pek[:], wscr['pek'][:, 0, :], ['b_pek'], ['pek'], 'c_pek')
            LOAD(pev[:], wscr['pev'][:, 0, :], ['b_pev'], ['pev'], 'c_pev')
            LOAD(rnnc[:], rnnc_d[:, :, :], [], ['rnnc'], 'c_rnnc')
            LOAD(hgn[:], hg_d[:, :], [], ['hgn'], 'c_hgn')
            LOAD(flags[:], flags_d[:, :], [], ['flags'], 'c_flags')
            state = sb('state', [BW, NB])
            rxhalo = sb('rxhalo', [BW, NB, 3])
            kchalo = sb('kchalo', [128, 4, 16], BF16)
            lcn = sb('lcn', [BW, NB, 2])
            DVE(lambda e: e.memset(state[:], 0.0), [], ['state'])
            DVE(lambda e: e.memset(rxhalo[:], 0.0), [], ['rxhalo'])
            DVE(lambda e: e.memset(kchalo[:], 0.0), [], ['kchalo'])
            tl = [sb('tl%d' % i, [BW, NB]) for i in range(4)]
            ACT(lambda e: e.activation(out=tl[0][:], in_=rnnc[:, :, 7], func=AF.Exp, scale=-1.0), ['rnnc'], ['tl0'])
            DVE(lambda e: e.tensor_scalar(out=tl[1][:], in0=tl[0][:], scalar1=2.0, scalar2=None, op0=ALU.add), ['tl0'], ['tl1'])
            DVE(lambda e: e.reciprocal(out=tl[2][:], in_=tl[1][:]), ['tl1'], ['tl2'])
            DVE(lambda e: e.tensor_tensor(out=tl[1][:], in0=tl[0][:], in1=tl[2][:], op=ALU.mult), ['tl0', 'tl2'], ['tl1'])
            DVE(lambda e: e.tensor_tensor(out=tl[2][:], in0=tl[1][:], in1=tl[1][:], op=ALU.mult), ['tl1'], ['tl2'])
            DVE(lambda e: e.tensor_scalar(out=tl[3][:], in0=tl[2][:], scalar1=1.0 / 9, scalar2=1.0 / 7, op0=ALU.mult, op1=ALU.add), ['tl2'], ['tl3'])
            for cst in (1.0 / 5, 1.0 / 3, 1.0):
                DVE(lambda e: e.tensor_tensor(out=tl[3][:], in0=tl[3][:], in1=tl[2][:], op=ALU.mult), ['tl3', 'tl2'], ['tl3'])
                DVE(lambda e, cst=cst: e.tensor_scalar(out=tl[3][:], in0=tl[3][:], scalar1=cst, scalar2=None, op0=ALU.add), ['tl3'], ['tl3'])
            DVE(lambda e: e.tensor_tensor(out=tl[3][:], in0=tl[3][:], in1=tl[1][:], op=ALU.mult), ['tl3', 'tl1'], ['tl3'])
            DVE(lambda e: e.tensor_scalar(out=lcn[:, :, 0], in0=tl[3][:], scalar1=-16.0, scalar2=None, op0=ALU.mult), ['tl3'], ['lcn'])

            A = [ps('A%d' % i) for i in range(3)]
            G = [ps('G%d' % i) for i in range(2)]
            Hh = ps('Hh')
            Hh2 = ps('Hh2')
            Tb = ps('Tb', BF16)
            acc_i = [0]

            def nextA():
                i = acc_i[0] % 3
                acc_i[0] += 1
                return A[i], 'A%d' % i
            hbias = sb('hbias', [128, 2])
            for kv, (w1, pe_) in enumerate(((w1k, pek), (w1v, pev))):
                for l in range(32):
                    PE(lambda e, l=l, w1=w1, pe_=pe_, kv=kv: e.matmul(Hh[:, kv:kv + 1], lhsT=w1[0:64, l, :], rhs=pe_[0:64, l:l + 1],
                                                                      start=(l == 0 and kv == 0), stop=(l == 31), skip_group_check=True),
                       ['w1k', 'w1v', 'pek', 'pev'], ['Hh'])
            ACT(lambda e: e.copy(out=hbias[:], in_=Hh[:, 0:2]), ['Hh'], ['hbias'])

            xt = [sb('xt%d' % i, [128, 4, D]) for i in range(2)]
            junk = sb('junk', [128, D], BF16)
            xn = sb('xn', [128, D], BF16)
            ssq = sb('ssq', [128, 8])
            hT = sb('hT', [128, 8, 512], BF16)
            hgT = sb('hgT', [BW, NB, 512], BF16)
            rxb = [sb('rxb%d' % i, [BW, 515]) for i in range(2)]
            xc = [sb('xc%d' % i, [BW, 512]) for i in range(2)]
            xcb = [sb('xcb%d' % i, [BW, 512], BF16) for i in range(2)]
            rr = [sb('rr%d' % i, [BW, 512]) for i in range(2)]
            ii = [sb('ii%d' % i, [BW, 512]) for i in range(2)]
            aa = [sb('aa%d' % i, [BW, 512]) for i in range(2)]
            mm_ = [sb('mm%d' % i, [BW, 512]) for i in range(2)]
            hh = [sb('hh%d' % i, [BW, 512]) for i in range(2)]
            kcT = sb('kcT', [128, 4, 528], BF16)
            sqb = sb('sqb', [128, 512], BF16)
            rstd = sb('rstd', [128, 512])
            stg = [sb('stg%d' % i, [128, 512], BF16) for i in range(4)]
            vstg = [sb('vstg%d' % i, [128, 2, 4, 65], BF16) for i in range(2)]
            gstg = [sb('gstg%d' % i, [128, 48]) for i in range(2)]
            hid = sb('hid', [128, 8, 32], BF16)
            vcs = sb('vcs', [32, 4, 64], BF16)
            sgb = sb('sgb', [128, 512])
            stg_i = [0]
            for i in range(2):
                DVE(lambda e, i=i: e.memset(vstg[i][:], 1.0), [], ['vstg%d' % i])

            def next_stg():
                i = stg_i[0] % 4
                stg_i[0] += 1
                return stg[i], 'stg%d' % i

            def load_x(ti):
                t0, W, mode = cfg['tiles'][ti]
                nsub = W // 128
                LOAD(xt[ti % 2][:, 0:nsub, :], xloc[t0:t0 + W, :].rearrange("(s p) d -> p s d", p=128),
                     [], ['xt%d' % (ti % 2)], 'xt%d' % (ti % 2))

            def rmsnorm_T(xtile, xname, nsub, dstT, dname):
                for s in range(nsub):
                    ACT(lambda e, s=s: e.activation(out=junk[:], in_=xtile[:, s, :], func=AF.Square, accum_out=ssq[:, s:s + 1]),
                        [xname], ['junk', 'ssq'])
                ACT(lambda e: e.activation(out=ssq[:, 4:4 + nsub], in_=ssq[:, 0:nsub], func=AF.Sqrt, scale=1.0 / D, bias=EPS), ['ssq'], ['ssq'])
                DVE(lambda e: e.reciprocal(out=ssq[:, 4:4 + nsub], in_=ssq[:, 4:4 + nsub]), ['ssq'], ['ssq'])
                for s in range(nsub):
                    ACT(lambda e, s=s: e.activation(out=xn[:], in_=xtile[:, s, :], func=AF.Copy, scale=ssq[:, 4 + s:5 + s]),
                        [xname, 'ssq'], ['xn'])
                    for k in range(8):
                        PE(lambda e, k=k: e.transpose(out=Tb[:, k * 128:(k + 1) * 128], in_=xn[:, k * 128:(k + 1) * 128], identity=ident[:]),
                           ['xn', 'ident'], ['Tb'])
                    DVE(lambda e, s=s: e.tensor_copy(out=dstT[:, :, s * 128:(s + 1) * 128],
                                                     in_=Tb[:, :].rearrange("p (k q) -> p k q", k=8)), ['Tb'], [dname])

            def headnorm(src_ps, src_name, W, gcol, dst, dname, extra_reads=()):
                ACT(lambda e: e.activation(out=sqb[:, :W], in_=src_ps[:, :W], func=AF.Square), [src_name], ['sqb'])
                PE(lambda e: e.matmul(G[1][:, :W], lhsT=bones[:], rhs=sqb[:, :W], start=True, stop=True), ['sqb', 'bones'], ['G1'])
                ACT(lambda e: e.activation(out=rstd[:, :W], in_=G[1][:, :W], func=AF.Sqrt, bias=EPS), ['G1'], ['rstd'])
                DVE(lambda e: e.reciprocal(out=rstd[:, :W], in_=rstd[:, :W]), ['rstd'], ['rstd'])
                DVE(lambda e: e.scalar_tensor_tensor(out=dst, in0=src_ps[:, :W], scalar=hgn[:, gcol:gcol + 1], in1=rstd[:, :W],
                                                     op0=ALU.mult, op1=ALU.mult), [src_name, 'rstd', 'hgn'], [dname])

            load_x(0)
            chk('setup')
            for ti, (t0, W, mode) in enumerate(cfg['tiles']):
                qside = mode != 'K'
                nsub = W // 128
                tq0 = t0 - (QS - 128)
                if ti + 1 < len(cfg['tiles']):
                    load_x(ti + 1)
                xtile, xname = xt[ti % 2], 'xt%d' % (ti % 2)
                rmsnorm_T(xtile, xname, nsub, hT, 'hT')
                chk('norm')
                if t0 == QS:
                    DVE(lambda e: e.tensor_scalar(out=state[:], in0=state[:], scalar1=flags[0:BW, 1:2], scalar2=None, op0=ALU.mult),
                        ['state', 'flags'], ['state'])
                for a in range(4):
                    wrx, wrxn = ws.next(('win', C_RX + 336 * a))
                    if qside:
                        wrg, wrgn = ws.next(('win', C_RG + 336 * a))
                    for cl in range(4):
                        c = 4 * a + cl
                        p = c % 2
                        pa, pan = nextA()
                        for k in range(8):
                            PE(lambda e, k=k, pa=pa, wrx=wrx, cl=cl: e.matmul(pa[0:BW, :W], lhsT=wrx[:, k, cl * BW:(cl + 1) * BW], rhs=hT[:, k, :W],
                                                                             start=(k == 0), stop=(k == 7)), [wrxn, 'hT'], [pan])
                        rb, rbn = rxb[p], 'rxb%d' % p
                        DVE(lambda e, rb=rb, c=c: e.tensor_copy(out=rb[:, 0:3], in_=rxhalo[:, c, :]), ['rxhalo'], [rbn])
                        ACT(lambda e, rb=rb, pa=pa: e.copy(out=rb[:, 3:3 + W], in_=pa[0:BW, :W]), [pan], [rbn])
                        ACT(lambda e, rb=rb, c=c: e.copy(out=rxhalo[:, c, :], in_=rb[:, W:W + 3]), [rbn], ['rxhalo'])
                        xcc, xcn = xc[p], 'xc%d' % p
                        ACT(lambda e, pa=pa, xcc=xcc, c=c: e.activation(out=xcc[:, :W], in_=pa[0:BW, :W], func=AF.Identity,
                                                                      scale=rnnc[:, c, 3:4], bias=rnnc[:, c, 4:5]), [pan, 'rnnc'], [xcn])
                        for k in range(3):
                            DVE(lambda e, k=k, rb=rb, xcc=xcc, c=c: e.scalar_tensor_tensor(
                                out=xcc[:, :W], in0=rb[:, k:k + W], scalar=rnnc[:, c, k:k + 1], in1=xcc[:, :W],
                                op0=ALU.mult, op1=ALU.add), [rbn, xcn, 'rnnc'], [xcn])
                        xb_, xbn = xcb[p], 'xcb%d' % p
                        ACT(lambda e, xb_=xb_, xcc=xcc: e.copy(out=xb_[:, :W], in_=xcc[:, :W]), [xcn], [xbn])
                        PE(lambda e, c=c, xb_=xb_: e.matmul(G[0][0:BW, :W], lhsT=rga[:, c, :], rhs=xb_[:, :W], start=True, stop=True), ['rga', xbn], ['G0'])
                        PE(lambda e, c=c, xb_=xb_: e.matmul(G[1][0:BW, :W], lhsT=rgx[:, c, :], rhs=xb_[:, :W], start=True, stop=True), ['rgx', xbn], ['G1'])
                        r_, rn = rr[p], 'rr%d' % p
                        i_, in_n = ii[p], 'ii%d' % p
                        ACT(lambda e, r_=r_, c=c: e.activation(out=r_[:, :W], in_=G[0][0:BW, :W], func=AF.Sigmoid, bias=rnnc[:, c, 5:6]), ['G0', 'rnnc'], [rn])
                        ACT(lambda e, i_=i_, c=c: e.activation(out=i_[:, :W], in_=G[1][0:BW, :W], func=AF.Sigmoid, bias=rnnc[:, c, 6:7]), ['G1', 'rnnc'], [in_n])
                        a_, an = aa[p], 'aa%d' % p
                        m_, mn = mm_[p], 'mm%d' % p
                        ACT(lambda e, a_=a_, r_=r_, c=c: e.activation(out=a_[:, :W], in_=r_[:, :W], func=AF.Exp, scale=lcn[:, c, 0:1]), [rn, 'lcn'], [an])
                        DVE(lambda e, a_=a_, m_=m_: e.tensor_tensor(out=m_[:, :W], in0=a_[:, :W], in1=a_[:, :W], op=ALU.mult), [an], [mn])
                        ACT(lambda e, m_=m_: e.activation(out=m_[:, :W], in_=m_[:, :W], func=AF.Sqrt, scale=-1.0, bias=1.0), [mn], [mn])
                        DVE(lambda e, i_=i_, xcc=xcc: e.tensor_tensor(out=i_[:, :W], in0=i_[:, :W], in1=xcc[:, :W], op=ALU.mult), [in_n, xcn], [in_n])
                        DVE(lambda e, i_=i_, m_=m_: e.tensor_tensor(out=i_[:, :W], in0=i_[:, :W], in1=m_[:, :W], op=ALU.mult), [in_n, mn], [in_n])
                        h_, hn = hh[p], 'hh%d' % p
                        DVE(lambda e, h_=h_, a_=a_, i_=i_, c=c: e.tensor_tensor_scan(out=h_[:, :W], data0=a_[:, :W], data1=i_[:, :W],
                                                                                     initial=state[:, c:c + 1], op0=ALU.mult, op1=ALU.add),
                            [an, in_n, 'state'], [hn])
                        ACT(lambda e, h_=h_, c=c: e.copy(out=state[:, c:c + 1], in_=h_[:, W - 1:W]), [hn], ['state'])
                        if qside:
                            pg, pgn = nextA()
                            for k in range(8):
                                PE(lambda e, k=k, pg=pg, wrg=wrg, cl=cl: e.matmul(pg[0:BW, :W], lhsT=wrg[:, k, cl * BW:(cl + 1) * BW], rhs=hT[:, k, :W],
                                                                                 start=(k == 0), stop=(k == 7)), [wrgn, 'hT'], [pgn])
                            ACT(lambda e, pg=pg, r_=r_: e.activation(out=r_[:, :W], in_=pg[0:BW, :W], func=AF.Gelu_apprx_tanh), [pgn], [rn])
                            DVE(lambda e, h_=h_, r_=r_, c=c: e.tensor_tensor(out=hgT[:, c, :W], in0=h_[:, :W], in1=r_[:, :W], op=ALU.mult), [hn, rn], ['hgT'])
                chk('rnn')
                wkc, wkcn = ws.next(('win', C_KC))
                nblk = W // 16
                for idx in range(4):
                    pa, pan = nextA()
                    for k in range(8):
                        PE(lambda e, k=k, pa=pa, idx=idx: e.matmul(pa[:, :W], lhsT=wkc[:, k, idx * 128:(idx + 1) * 128], rhs=hT[:, k, :W],
                                                                   start=(k == 0), stop=(k == 7)), [wkcn, 'hT'], [pan])
                    DVE(lambda e, idx=idx: e.tensor_copy(out=kcT[:, idx, 0:16], in_=kchalo[:, idx, :]), ['kchalo'], ['kcT'])
                    ACT(lambda e, pa=pa, idx=idx: e.copy(out=kcT[:, idx, 16:16 + W], in_=pa[:, :W]), [pan], ['kcT'])
                    ACT(lambda e, idx=idx: e.copy(out=kchalo[:, idx, :], in_=kcT[:, idx, W:W + 16]), ['kcT'], ['kchalo'])
                chk('cmp1')
                for kv in range(2):
                    w1 = w1k if kv == 0 else w1v
                    for hf in range(2):
                        HB, hbn = (Hh, 'Hh') if hf == 0 else (Hh2, 'Hh2')
                        for G_ in range(2):
                            base = kcT[64 * hf:64 * hf + 64, kv * 2 + G_, 0:1]
                            for l in range(32):
                                rhs = bass.AP(base.tensor, base.offset + l, [list(base.ap[0]), [16, nblk]])
                                PE(lambda e, l=l, rhs=rhs, w1=w1, hf=hf, G_=G_, HB=HB: e.matmul(
                                    HB[:, G_ * 32:G_ * 32 + nblk], lhsT=w1[64 * hf:64 * hf + 64, l, :], rhs=rhs,
                                    start=(l == 0 and G_ == 0), stop=(l == 31), skip_group_check=True),
                                   ['kcT', 'w1k', 'w1v'], [hbn])
                    for hf in range(2):
                        HB, hbn = (Hh, 'Hh') if hf == 0 else (Hh2, 'Hh2')
                        ACT(lambda e, kv=kv, hf=hf, HB=HB: e.activation(
                            out=hid[:, kv * 4:kv * 4 + 4, :].rearrange("p (G h) b -> p h G b", h=2)[:, hf, :, 0:nblk],
                            in_=HB[:, 0:64].rearrange("p (G b) -> p G b", G=2)[:, :, 0:nblk],
                            func=AF.Gelu_apprx_tanh, bias=hbias[:, kv:kv + 1]), [hbn, 'hbias'], ['hid'])
                chk('cmp2')
                b0 = 1 if t0 == 0 else 0
                cfirst = t0 // 16 - 1 + b0
                nst = nblk - b0
                for G_ in range(2):
                    for hf in range(2):
                        g = 2 * G_ + hf
                        PE(lambda e, g=g, hf=hf: e.matmul(G[0][64 * hf:64 * hf + 64, 0:nblk], lhsT=w2k[:, :], rhs=hid[:, g, 0:nblk],
                                                          start=True, stop=True), ['w2k', 'hid'], ['G0'])
                    st_, stn = next_stg()
                    headnorm(G[0], 'G0', nblk, 1, st_[:, 0:nblk], stn)
                    STORE(KCT[:, G_, cfirst:cfirst + nst], st_[:, b0:nblk], [stn], ['KCT'], stn)
                chk('cmp3')
                for g in range(4):
                    PE(lambda e, g=g: e.matmul(G[0][0:nblk, g * 64:(g + 1) * 64], lhsT=hid[:, 4 + g, 0:nblk], rhs=w2v[:, :],
                                               start=True, stop=True), ['w2v', 'hid'], ['G0'])
                ACT(lambda e: e.copy(out=vcs[0:nblk, :, :], in_=G[0][0:nblk, 0:256].rearrange("p (g d) -> p g d", g=4)), ['G0'], ['vcs'])
                if b0 == 0:
                    STORE(VCM[cfirst:cfirst + nst, :, :], vcs[0:nblk, :, :], ['vcs'], ['VCM'], 'vcs')
                else:
                    STORE(VCM[cfirst:cfirst + nst, :, :], vcs[1:nblk, :, :], ['vcs'], ['VCM'], 'vcs')
                chk('cmp')
                wks, wksn = ws.next(('win', C_KS))
                for idx in range(4):
                    pa, pan = nextA()
                    for k in range(8):
                        PE(lambda e, k=k, pa=pa, idx=idx: e.matmul(pa[:, :W], lhsT=wks[:, k, idx * 128:(idx + 1) * 128], rhs=hT[:, k, :W],
                                                                   start=(k == 0), stop=(k == 7)), [wksn, 'hT'], [pan])
                    st_, stn = next_stg()
                    headnorm(pa, pan, W, 2 if idx < 2 else 3, st_[:, :W], stn)
                    dst = KST if idx < 2 else KWT
                    STORE(dst[:, idx % 2, t0:t0 + W], st_[:, :W], [stn], ['KST' if idx < 2 else 'KWT'], stn)
                chk('ks')
                wvs, wvsn = ws.next(('win', C_VS))
                for s in range(nsub):
                    pa, pan = nextA()
                    for k in range(8):
                        PE(lambda e, k=k, pa=pa, s=s: e.matmul(pa[:, 0:512], lhsT=hT[:, k, s * 128:(s + 1) * 128], rhs=wvs[:, k, :],
                                                               start=(k == 0), stop=(k == 7)), [wvsn, 'hT'], [pan])
                    vs_, vsn = vstg[s % 2], 'vstg%d' % (s % 2)
                    ACT(lambda e, pa=pa, vs_=vs_: e.copy(out=vs_[:, :, :, 0:64], in_=pa[:, 0:512].rearrange("p (a g d) -> p a g d", a=2, g=4)),
                        [pan], [vsn])
                    if t0 < QS:
                        DVE(lambda e, vs_=vs_: e.memset(vs_[:, :, :, 64:65], 1.0), [], [vsn])
                        DVE(lambda e, vs_=vs_: e.tensor_scalar(out=vs_[:], in0=vs_[:], scalar1=flags[:, 0:1], scalar2=None, op0=ALU.mult),
                            [vsn, 'flags'], [vsn])
                    elif t0 == QS and s < 2:
                        DVE(lambda e, vs_=vs_: e.memset(vs_[:, :, :, 64:65], 1.0), [], [vsn])
                    r0 = t0 + s * 128
                    STORE(VS[r0:r0 + 128, :, :], vs_[:, 0, :, :], [vsn], ['VS'], vsn + 'a')
                    STORE(VW[r0:r0 + 128, :, :], vs_[:, 1, :, :], [vsn], ['VW'], vsn + 'b')
                chk('vs')
                if not qside:
                    continue
                for half in range(2):
                    wq, wqn = ws.next(('win', C_Q + 512 * half))
                    for mloc in range(4):
                        m = 4 * half + mloc
                        pa, pan = nextA()
                        for k in range(8):
                            PE(lambda e, k=k, pa=pa, mloc=mloc, wq=wq: e.matmul(pa[:, :W], lhsT=wq[:, k, mloc * 128:(mloc + 1) * 128], rhs=hT[:, k, :W],
                                                                                start=(k == 0), stop=(k == 7)), [wqn, 'hT'], [pan])
                        st_, stn = next_stg()
                        headnorm(pa, pan, W, 0, st_[:, :W], stn)
                        STORE(QT[:, m, tq0:tq0 + W], st_[:, :W], [stn], ['QT'], stn)
                wng, wngn = ws.next(('win', C_NG))
                for s in range(nsub):
                    pa, pan = nextA()
                    for k in range(8):
                        PE(lambda e, k=k, pa=pa, s=s: e.matmul(pa[:, 0:48], lhsT=hT[:, k, s * 128:(s + 1) * 128], rhs=wng[:, k, :],
                                                               start=(k == 0), stop=(k == 7)), [wngn, 'hT'], [pan])
                    gs_, gsn = gstg[s % 2], 'gstg%d' % (s % 2)
                    ACT(lambda e, pa=pa, gs_=gs_: e.activation(out=gs_[:], in_=pa[:, 0:48], func=AF.Sigmoid), [pan], [gsn])
                    STORE(GATES[tq0 + s * 128:tq0 + (s + 1) * 128, :], gs_[:], [gsn], ['GATES'], gsn)
                for jj in range(4):
                    wmr, wmrn = ws.next(('win', C_MR + 256 * jj))
                    wro, wron = ws.next(('wro', 256 * jj))
                    for jl in range(2):
                        j = 2 * jj + jl
                        pb, pbn = nextA()
                        for k in range(8):
                            PE(lambda e, k=k, pb=pb, jl=jl, wmr=wmr: e.matmul(pb[:, :W], lhsT=wmr[:, k, jl * 128:(jl + 1) * 128], rhs=hT[:, k, :W],
                                                                             start=(k == 0), stop=(k == 7)), [wmrn, 'hT'], [pbn])
                        ACT(lambda e, pb=pb: e.activation(out=sgb[:, :W], in_=pb[:, :W], func=AF.Sigmoid), [pbn], ['sgb'])
                        pa, pan = nextA()
                        for c in range(NB):
                            PE(lambda e, c=c, pa=pa, jl=jl, wro=wro: e.matmul(pa[:, :W], lhsT=wro[:, c, jl * 128:(jl + 1) * 128], rhs=hgT[:, c, :W],
                                                                             start=(c == 0), stop=(c == NB - 1)), [wron, 'hgT'], [pan])
                        st_, stn = next_stg()
                        DVE(lambda e, pa=pa, st_=st_: e.tensor_tensor(out=st_[:, :W], in0=pa[:, :W], in1=sgb[:, :W], op=ALU.mult), [pan, 'sgb'], [stn])
                        STORE(T1T[:, j, tq0:tq0 + W], st_[:, :W], [stn], ['T1T'], stn)
                for half in range(2):
                    wma, wman = ws.next(('win', C_MA + 512 * half))
                    for jl in range(4):
                        j = 4 * half + jl
                        pb, pbn = nextA()
                        for k in range(8):
                            PE(lambda e, k=k, pb=pb, jl=jl, wma=wma: e.matmul(pb[:, :W], lhsT=wma[:, k, jl * 128:(jl + 1) * 128], rhs=hT[:, k, :W],
                                                                             start=(k == 0), stop=(k == 7)), [wman, 'hT'], [pbn])
                        st_, stn = next_stg()
                        ACT(lambda e, pb=pb, st_=st_: e.activation(out=st_[:, :W], in_=pb[:, :W], func=AF.Sigmoid), [pbn], [stn])
                        STORE(SGAT[:, j, tq0:tq0 + W], st_[:, :W], [stn], ['SGAT'], stn)
            S.barrier()

        if upto == 1:
            S.emit()
            return nc
        with ExitStack() as es:
            def sb(name, shape, dt=F32):
                return es.enter_context(nc.sbuf_tensor('s2_' + name, list(shape), dt))

            def ps(name, dt=F32):
                return es.enter_context(nc.psum_tensor('p1_' + name, [128, 512 if dt == F32 else 1024], dt))
            ksT = sb('ksT', [128, 2, L], BF16)
            vsa = sb('vsa', [128, NCH, 4, 65], BF16)
            kcmp = sb('kcmp', [128, 2, NCC * 128], BF16)
            vcm = sb('vcm', [128, NCC, 4, 65], BF16)
            ovl = sb('ovl', [128, NCC, NS], BF16)
            validc = sb('validc', [128, NCC])
            cmask = sb('cmask', [128, 17, 128], BF16)
            tri = sb('tri', [128, 2, 128], BF16)
            wexp = sb('wexp', [NS, L], BF16)
            ident = sb('ident2', [128, 128], BF16)
            DVE(lambda e: e.memset(kcmp[:], 0.0), [], ['kcmp'])
            DVE(lambda e: e.memset(vcm[:], 0.0), [], ['vcm'])
            nkc = cfg['NCB']
            LOAD(kcmp[:, :, 0:nkc], KCT[:, :, 0:nkc], [], ['kcmp'], 'c2_kcmp')
            for cc in range(NCC):
                nr = min(128, nkc - cc * 128)
                LOAD(vcm[0:nr, cc, :, 0:64], VCM[cc * 128:cc * 128 + nr, :, :], [], ['vcm'], 'c2_vcm%d' % cc)
            LOAD(validc[:], validc_d[:, :], [], ['validc'], 'c2_validc')
            for cc in range(NCC):
                DVE(lambda e, cc=cc: e.memset(vcm[:, cc, :, 64:65], 1.0), ['vcm'], ['vcm'])
                DVE(lambda e, cc=cc: e.tensor_scalar(out=vcm[:, cc, :, :], in0=vcm[:, cc, :, :], scalar1=validc[:, cc:cc + 1], scalar2=None,
                                                     op0=ALU.mult), ['vcm', 'validc'], ['vcm'])
            LOAD(ovl[:], ovl_d[:, :, :], [], ['ovl'], 'c2_ovl')
            LOAD(cmask[:], cmask_d[:, :, :], [], ['cmask'], 'c2_cmask')
            LOAD(tri[:], tri_d[:, :, :], [], ['tri'], 'c2_tri')
            LOAD(wexp[:], wexp_d[:, :], [], ['wexp'], 'c2_wexp')
            LOAD(ident[:], ident_d[:, :], [], ['ident2'], 'c2_ident')
            nld = 8
            for a in range(nld):
                c0, c1 = a * L // nld, (a + 1) * L // nld
                LOAD(ksT[:, :, c0:c1], KST[:, :, c0:c1], [], ['ksT'], 'c2_ks%d' % a)
                j0, j1 = a * NCH // nld, (a + 1) * NCH // nld
                LOAD(vsa[:, j0:j1, :, :], VS[j0 * 128:j1 * 128, :, :].rearrange("(j p) g d -> p j g d", p=128), [], ['vsa'], 'c2_vs%d' % a)

            Sb = [ps('S%d' % i) for i in range(2)]
            Mbs = [ps('Mb0'), ps('Mb1')]
            OS = ps('OS')
            OCM = ps('OC')
            OW = OCM
            IMP = ps('IMP')
            Tb = ps('Tb2', BF16)
            qt = [sb('qt%d' % i, [128, 8, 128], BF16) for i in range(2)]
            gt = [sb('gt%d' % i, [128, 48]) for i in range(2)]
            rkc = [sb('rkc%d' % i, [128, 2 * NS]) for i in range(2)]
            kwt = [sb('kwt%d' % i, [128, 2, 640], BF16) for i in range(2)]
            vwt = [sb('vwt%d' % i, [128, 5, 4, 65], BF16) for i in range(2)]
            Eb = [sb('E%d' % i, [128, 512], BF16) for i in range(3)]
            Pb = [sb('P%d' % i, [128, 512], BF16) for i in range(3)]
            otok = [sb('otok%d' % i, [128, D]) for i in range(2)]
            obf = sb('obf', [128, D], BF16)
            oTs = [sb('oTs%d' % i, [128, 8, 128], BF16) for i in range(2)]
            rz = sb('rz', [128, 4])
            wg = sb('wg', [128, 4])
            tmpo = sb('tmpo', [128, 256])
            acc = sb('acc', [128, NS])
            rank = sb('rank', [128, NS])
            rank2 = sb('rank2', [128, NS])
            m8a = sb('m8a', [128, 8])
            m8b = sb('m8b', [128, 8])
            sel = sb('sel', [128, NS], BF16)
            selT = sb('selT', [NS, 128], BF16)
            ectr = [0]

            def load_q(il):
                p = il % 2
                tq = il * 128
                t0 = QS - 128 + tq
                LOAD(qt[p][:], QT[:, :, tq:tq + 128], [], ['qt%d' % p], 'qt%d' % p)
                LOAD(gt[p][:], GATES[tq:tq + 128, :], [], ['gt%d' % p], 'gt%d' % p)
                LOAD(rkc[p][:], rankc_d[il, :, :], [], ['rkc%d' % p], 'rkc%d' % p)
                LOAD(kwt[p][:], KWT[:, :, t0 - 512:t0 + 128], [], ['kwt%d' % p], 'kwt%d' % p)
                LOAD(vwt[p][:], VW[t0 - 512:t0 + 128, :, :].rearrange("(j p) g d -> p j g d", p=128), [], ['vwt%d' % p], 'vwt%d' % p)

            def chunk_pipeline(items, il, g, OB, obn):
                p = il % 2
                G_, hf = g // 2, g % 2
                qrhs = qt[p][64 * hf:64 * hf + 64, 4 * G_:4 * G_ + 4, :]
                n = len(items)
                st = {}

                def front(idx):
                    it = items[idx]
                    sbk = idx % 2
                    PE(lambda e: e.matmul(Sb[sbk][:, :].rearrange("p (r q) -> p r q", r=4), lhsT=it['lhsT'], rhs=qrhs, start=True, stop=True),
                       it['kreads'] + ['qt%d' % p], ['S%d' % sbk])
                    if it.get('selmask') is not None:
                        j = it['selmask']
                        PE(lambda e: e.matmul(Mbs[sbk][:, 0:128], lhsT=wexp[:, j * 128:(j + 1) * 128], rhs=selT[:, :],
                                              start=True, stop=True), ['wexp', 'selT'], ['Mb%d' % sbk])
                    ei = ectr[0] % 3
                    ectr[0] += 1
                    ACT(lambda e: e.activation(out=Eb[ei][:], in_=Sb[sbk][:, :], func=AF.Exp, scale=0.125), ['S%d' % sbk], ['E%d' % ei])
                    src, srcn = Eb[ei], 'E%d' % ei
                    if it.get('selmask') is not None:
                        DVE(lambda e: e.tensor_tensor(out=Pb[ei][:].rearrange("p (r q) -> p r q", r=4), in0=Eb[ei][:].rearrange("p (r q) -> p r q", r=4),
                                                      in1=Mbs[sbk][:, 0:128].unsqueeze(1).broadcast_to([128, 4, 128]), op=ALU.mult),
                            ['E%d' % ei, 'Mb%d' % sbk], ['P%d' % ei])
                        src, srcn = Pb[ei], 'P%d' % ei
                    elif it.get('mask') is not None:
                        mk, mkn = it['mask']
                        DVE(lambda e: e.tensor_tensor(out=Pb[ei][:].rearrange("p (r q) -> p r q", r=4), in0=Eb[ei][:].rearrange("p (r q) -> p r q", r=4),
                                                      in1=mk.unsqueeze(1).broadcast_to([128, 4, 128]), op=ALU.mult),
                            ['E%d' % ei, mkn], ['P%d' % ei])
                        src, srcn = Pb[ei], 'P%d' % ei
                    st[idx] = (src, srcn)

                def back(idx):
                    it = items[idx]
                    src, srcn = st[idx]
                    for r in range(4):
                        for (ob, obname, rhs, rreads, width) in it['outs']:
                            PE(lambda e, r=r, ob=ob, rhs=rhs, width=width, first=(idx == 0 and r == 0): e.matmul(
                                ob[:, r * width:(r + 1) * width], lhsT=src[:, r * 128:(r + 1) * 128], rhs=rhs,
                                start=first, stop=(idx == n - 1), skip_group_check=True), [srcn] + rreads, [obname])
                front(0)
                for idx in range(n):
                    if idx + 1 < n:
                        front(idx + 1)
                    back(idx)

            def finish_branch(il, g, OB, obn, e_idx, first):
                p = il % 2
                ot, otn = otok[p], 'otok%d' % p
                ob3 = OB[:, 0:260].rearrange("p (r d) -> p r d", r=4)
                g3 = gt[p][:, :].rearrange("p (h e) -> p h e", e=3)
                DVE(lambda e: e.tensor_scalar(out=rz[:], in0=ob3[:, :, 64], scalar1=1e-30, scalar2=None, op0=ALU.max), [obn], ['rz'])
                DVE(lambda e: e.reciprocal(out=rz[:], in_=rz[:]), ['rz'], ['rz'])
                DVE(lambda e: e.tensor_tensor(out=wg[:], in0=rz[:], in1=g3[:, 4 * g:4 * g + 4, e_idx], op=ALU.mult), ['rz', 'gt%d' % p], ['wg'])
                dst = ot[:, 256 * g:256 * (g + 1)].rearrange("p (r d) -> p r d", r=4)
                if first:
                    DVE(lambda e: e.tensor_tensor(out=dst, in0=ob3[:, :, 0:64], in1=wg[:, :].unsqueeze(2).broadcast_to([128, 4, 64]), op=ALU.mult),
                        [obn, 'wg'], [otn])
                else:
                    DVE(lambda e: e.tensor_tensor(out=tmpo[:].rearrange("p (r d) -> p r d", r=4), in0=ob3[:, :, 0:64],
                                                  in1=wg[:, :].unsqueeze(2).broadcast_to([128, 4, 64]), op=ALU.mult), [obn, 'wg'], ['tmpo'])
                    DVE(lambda e: e.tensor_tensor(out=ot[:, 256 * g:256 * (g + 1)], in0=ot[:, 256 * g:256 * (g + 1)], in1=tmpo[:], op=ALU.add),
                        [otn, 'tmpo'], [otn])

            load_q(0)
            for il in range(NQT):
                if il > 0 and il % 6 == 0:
                    S.barrier()
                if il + 1 < NQT:
                    load_q(il + 1)
                p = il % 2
                i = (QS - 128) // 128 + il
                for g in range(4):
                    G_, hf = g // 2, g % 2
                    ncc = i // 16 + 1
                    items = []
                    for cc in range(ncc):
                        mask = None
                        if cc == i // 16:
                            mask = (cmask[:, i % 16, :], 'cmask')
                        elif cc == i // 16 - 1 and i % 16 == 0:
                            mask = (cmask[:, 16, :], 'cmask')
                        items.append(dict(lhsT=kcmp[64 * hf:64 * hf + 64, G_, cc * 128:(cc + 1) * 128], kreads=['kcmp'], mask=mask,
                                          outs=[(OCM, 'OC', vcm[:, cc, g, :], ['vcm'], 65), (IMP, 'IMP', ovl[:, cc, :], ['ovl'], NS)]))
                    chunk_pipeline(items, il, g, OCM, 'OC')
                    finish_branch(il, g, OCM, 'OC', 0, True)
                    for r in range(4):
                        DVE(lambda e, r=r: e.scalar_tensor_tensor(out=acc[:], in0=IMP[:, r * NS:(r + 1) * NS], scalar=rz[:, r:r + 1],
                                                                  in1=(rkc[p][:, 0:NS] if r == 0 else acc[:]), op0=ALU.mult, op1=ALU.add),
                            ['IMP', 'rz', 'rkc%d' % p, 'acc'], ['acc'])
                    DVE(lambda e: e.tensor_tensor(out=rank[:], in0=acc[:], in1=rkc[p][:, NS:2 * NS], op=ALU.mult), ['acc', 'rkc%d' % p], ['rank'])
                    DVE(lambda e: e.max(out=m8a[:], in_=rank[:]), ['rank'], ['m8a'])
                    DVE(lambda e: e.match_replace(out=rank2[:], in_to_replace=m8a[:], in_values=rank[:], imm_value=-1.0), ['rank', 'm8a'], ['rank2'])
                    DVE(lambda e: e.max(out=m8b[:], in_=rank2[:]), ['rank2'], ['m8b'])
                    DVE(lambda e: e.tensor_scalar(out=sel[:], in0=rank[:], scalar1=m8b[:, 7:8], scalar2=None, op0=ALU.is_ge), ['rank', 'm8b'], ['sel'])
                    PE(lambda e: e.transpose(out=Tb[0:NS, 0:128], in_=sel[:, :], identity=ident[:]), ['sel', 'ident2'], ['Tb2'])
                    ACT(lambda e: e.copy(out=selT[:, :], in_=Tb[0:NS, 0:128]), ['Tb2'], ['selT'])
                    items = []
                    for j in range(i + 1):
                        it = dict(lhsT=ksT[64 * hf:64 * hf + 64, G_, j * 128:(j + 1) * 128], kreads=['ksT'],
                                  outs=[(OS, 'OS', vsa[:, j, g, :], ['vsa'], 65)])
                        if j < i:
                            it['selmask'] = j
                        else:
                            it['mask'] = (tri[:, 0, :], 'tri')
                        items.append(it)
                    chunk_pipeline(items, il, g, OS, 'OS')
                    finish_branch(il, g, OS, 'OS', 1, False)
                    items = []
                    for jl in range(5):
                        mask = None
                        if jl == 4:
                            mask = (tri[:, 0, :], 'tri')
                        elif jl == 0:
                            mask = (tri[:, 1, :], 'tri')
                        items.append(dict(lhsT=kwt[p][64 * hf:64 * hf + 64, G_, jl * 128:(jl + 1) * 128], kreads=['kwt%d' % p], mask=mask,
                                          outs=[(OW, 'OC', vwt[p][:, jl, g, :], ['vwt%d' % p], 65)]))
                    chunk_pipeline(items, il, g, OW, 'OC')
                    finish_branch(il, g, OW, 'OC', 2, False)
                ACT(lambda e, p=p: e.copy(out=obf[:], in_=otok[p][:]), ['otok%d' % p], ['obf'])
                for m in range(8):
                    PE(lambda e, m=m: e.transpose(out=Tb[:, m * 128:(m + 1) * 128], in_=obf[:, m * 128:(m + 1) * 128], identity=ident[:]),
                       ['obf', 'ident2'], ['Tb2'])
                ACT(lambda e, p=p: e.copy(out=oTs[p][:], in_=Tb[:, :].rearrange("p (m q) -> p m q", m=8)), ['Tb2'], ['oTs%d' % p])
                tq = il * 128
                STORE(OT[:, 0:4, tq:tq + 128], oTs[p][:, 0:4, :], ['oTs%d' % p], ['OT'], 'oTs%da' % p)
                STORE(OT[:, 4:8, tq:tq + 128], oTs[p][:, 4:8, :], ['oTs%d' % p], ['OT'], 'oTs%db' % p)
            S.barrier()

        if upto == 2:
            S.emit()
            return nc
        with ExitStack() as es:
            def sb(name, shape, dt=F32):
                return es.enter_context(nc.sbuf_tensor('s3_' + name, list(shape), dt))

            def ps(name, dt=F32):
                return es.enter_context(nc.psum_tensor('p2_' + name, [128, 512 if dt == F32 else 1024], dt))
            ptiles = [(QS - 128, 128, 'H')] + [(t, 512, 'Q') for t in range(QS, L, 512)]
            plan = []
            for (t0, W, mode) in ptiles:
                plan += [('wao', 0, 512), ('wao', 512, 512), ('wout', 0, 512), ('wout', 512, 512)]
                for jj in range(6):
                    plan += [('wup', 512 * jj, 512), ('wup', D_FF + 512 * jj, 512)]
                if mode == 'Q':
                    for c in range(2):
                        for jr in range(3):
                            plan.append(('wdn', c * 512, 512, jr))
            ident = sb('ident3', [128, 128], BF16)
            ffnc = sb('ffnc', [128, 48, 4])
            LOAD(ident[:], ident_d[:, :], [], ['ident3'], 'c3_ident')
            LOAD(ffnc[:], ffnc_d[:, :, :], [], ['ffnc'], 'c3_ffnc')
            slots = [sb('w3s%d' % i, [128, 4096], BF16) for i in range(4)]
            wst = dict(issued=0, taken=0)

            def w_issue():
                i = wst['issued']
                pc = plan[i]
                sl = i % 4
                view = slots[sl][:, :].rearrange("p (k n) -> p k n", k=8)
                if pc[0] == 'wdn':
                    src = wscr['wdn'][:, 8 * pc[3]:8 * pc[3] + 8, pc[1]:pc[1] + 512]
                else:
                    src = wscr[pc[0]][:, :, pc[1]:pc[1] + 512]
                LOAD(view, src, [], ['w3s%d' % sl], 'w3s%d' % sl)
                wst['issued'] += 1

            def w_next(expect):
                while wst['issued'] < min(len(plan), wst['taken'] + 3):
                    w_issue()
                i = wst['taken']
                assert plan[i][:2] == expect, (plan[i], expect)
                wst['taken'] += 1
                sl = i % 4
                return slots[sl][:, :].rearrange("p (k n) -> p k n", k=8), 'w3s%d' % sl

            A = [ps('A3_%d' % i) for i in range(3)]
            Dn = [ps('Dn%d' % i) for i in range(4)]
            Tb = ps('Tb3', BF16)
            acc_i = [0]

            def nextA():
                i = acc_i[0] % 3
                acc_i[0] += 1
                return A[i], 'A3_%d' % i
            xt = sb('x3', [128, 4, D])
            oT = sb('oT3', [128, 8, 512], BF16)
            t1 = sb('t13', [128, 8, 512], BF16)
            sga = sb('sga3', [128, 8, 512], BF16)
            mT = sb('mT3', [128, 8, 512], BF16)
            h2T = sb('h2T', [128, 8, 512], BF16)
            actT = sb('actT', [128, 24, 512], BF16)
            junk = sb('junk3', [128, D], BF16)
            xn = sb('xn3', [128, D], BF16)
            ssq = sb('ssq3', [128, 8])
            tmpm = sb('tmpm', [128, 512])
            rawb = [sb('rawb%d' % i, [128, 514]) for i in range(2)]
            cv = [sb('cv%d' % i, [128, 512]) for i in range(2)]
            gl = sb('gl', [128, 512])
            fhalo = sb('fhalo', [128, 48, 2])
            DVE(lambda e: e.memset(fhalo[:], 0.0), [], ['fhalo'])

            for ti, (t0, W, mode) in enumerate(ptiles):
                nsub = W // 128
                tq0 = t0 - (QS - 128)
                LOAD(xt[:, 0:nsub, :], xloc[t0:t0 + W, :].rearrange("(s p) d -> p s d", p=128), [], ['x3'], 'x3')
                LOAD(oT[:, :, :W], OT[:, :, tq0:tq0 + W], [], ['oT3'], 'oT3')
                LOAD(t1[:, :, :W], T1T[:, :, tq0:tq0 + W], [], ['t13'], 't13')
                LOAD(sga[:, :, :W], SGAT[:, :, tq0:tq0 + W], [], ['sga3'], 'sga3')
                for half in range(2):
                    w_, wn = w_next(('wao', 512 * half))
                    for jl in range(4):
                        j = 4 * half + jl
                        pa, pan = nextA()
                        for m in range(8):
                            PE(lambda e, m=m, pa=pa, jl=jl, w_=w_: e.matmul(pa[:, :W], lhsT=w_[:, m, jl * 128:(jl + 1) * 128], rhs=oT[:, m, :W],
                                                                           start=(m == 0), stop=(m == 7)), [wn, 'oT3'], [pan])
                        DVE(lambda e, pa=pa, j=j: e.tensor_tensor(out=tmpm[:, :W], in0=pa[:, :W], in1=sga[:, j, :W], op=ALU.mult), [pan, 'sga3'], ['tmpm'])
                        DVE(lambda e, j=j: e.tensor_tensor(out=mT[:, j, :W], in0=tmpm[:, :W], in1=t1[:, j, :W], op=ALU.add), ['tmpm', 't13'], ['mT3'])
                for c in range(2):
                    w_, wn = w_next(('wout', 512 * c))
                    for s in range(nsub):
                        pa, pan = nextA()
                        for j in range(8):
                            PE(lambda e, j=j, pa=pa, s=s, w_=w_: e.matmul(pa[:, 0:512], lhsT=mT[:, j, s * 128:(s + 1) * 128], rhs=w_[:, j, :],
                                                                         start=(j == 0), stop=(j == 7)), [wn, 'mT3'], [pan])
                        DVE(lambda e, pa=pa, s=s, c=c: e.tensor_tensor(out=xt[:, s, c * 512:(c + 1) * 512], in0=pa[:, 0:512],
                                                                       in1=xt[:, s, c * 512:(c + 1) * 512], op=ALU.add), [pan, 'x3'], ['x3'])
                for s in range(nsub):
                    ACT(lambda e, s=s: e.activation(out=junk[:], in_=xt[:, s, :], func=AF.Square, accum_out=ssq[:, s:s + 1]), ['x3'], ['junk3', 'ssq3'])
                ACT(lambda e: e.activation(out=ssq[:, 4:4 + nsub], in_=ssq[:, 0:nsub], func=AF.Sqrt, scale=1.0 / D, bias=EPS), ['ssq3'], ['ssq3'])
                DVE(lambda e: e.reciprocal(out=ssq[:, 4:4 + nsub], in_=ssq[:, 4:4 + nsub]), ['ssq3'], ['ssq3'])
                for s in range(nsub):
                    ACT(lambda e, s=s: e.activation(out=xn[:], in_=xt[:, s, :], func=AF.Copy, scale=ssq[:, 4 + s:5 + s]), ['x3', 'ssq3'], ['xn3'])
                    for k in range(8):
                        PE(lambda e, k=k: e.transpose(out=Tb[:, k * 128:(k + 1) * 128], in_=xn[:, k * 128:(k + 1) * 128], identity=ident[:]),
                           ['xn3', 'ident3'], ['Tb3'])
                    DVE(lambda e, s=s: e.tensor_copy(out=h2T[:, :, s * 128:(s + 1) * 128], in_=Tb[:, :].rearrange("p (k q) -> p k q", k=8)),
                        ['Tb3'], ['h2T'])
                for jj in range(6):
                    wg_, wgn = w_next(('wup', 512 * jj))
                    wv_, wvn = w_next(('wup', D_FF + 512 * jj))
                    for jl in range(4):
                        j = 4 * jj + jl
                        cvs = []
                        for which, (w_, wn) in enumerate(((wg_, wgn), (wv_, wvn))):
                            cidx = j + 24 * which
                            pa, pan = nextA()
                            for k in range(8):
                                PE(lambda e, k=k, pa=pa, jl=jl, w_=w_: e.matmul(pa[:, :W], lhsT=w_[:, k, jl * 128:(jl + 1) * 128], rhs=h2T[:, k, :W],
                                                                               start=(k == 0), stop=(k == 7)), [wn, 'h2T'], [pan])
                            rb, rbn = rawb[which], 'rawb%d' % which
                            DVE(lambda e, rb=rb, cidx=cidx: e.tensor_copy(out=rb[:, 0:2], in_=fhalo[:, cidx, :]), ['fhalo'], [rbn])
                            ACT(lambda e, rb=rb, pa=pa: e.copy(out=rb[:, 2:2 + W], in_=pa[:, :W]), [pan], [rbn])
                            ACT(lambda e, rb=rb, cidx=cidx: e.copy(out=fhalo[:, cidx, :], in_=rb[:, W:W + 2]), [rbn], ['fhalo'])
                            if mode == 'H':
                                continue
                            cc_, ccn = cv[which], 'cv%d' % which
                            ACT(lambda e, pa=pa, cc_=cc_, cidx=cidx: e.activation(out=cc_[:, :W], in_=pa[:, :W], func=AF.Identity,
                                                                                scale=ffnc[:, cidx, 2:3], bias=ffnc[:, cidx, 3:4]), [pan, 'ffnc'], [ccn])
                            for k in range(2):
                                DVE(lambda e, k=k, rb=rb, cc_=cc_, cidx=cidx: e.scalar_tensor_tensor(
                                    out=cc_[:, :W], in0=rb[:, k:k + W], scalar=ffnc[:, cidx, k:k + 1], in1=cc_[:, :W],
                                    op0=ALU.mult, op1=ALU.add), [rbn, ccn, 'ffnc'], [ccn])
                        if mode == 'H':
                            continue
                        ACT(lambda e: e.activation(out=gl[:, :W], in_=cv[0][:, :W], func=AF.Gelu_apprx_tanh), ['cv0'], ['gl'])
                        DVE(lambda e, j=j: e.tensor_tensor(out=actT[:, j, :W], in0=gl[:, :W], in1=cv[1][:, :W], op=ALU.mult), ['gl', 'cv1'], ['actT'])
                if mode == 'H':
                    continue
                for c in range(2):
                    for jr in range(3):
                        w_, wn = w_next(('wdn', 512 * c))
                        for s in range(nsub):
                            for k in range(8):
                                j = 8 * jr + k
                                PE(lambda e, j=j, k=k, s=s, w_=w_, jr=jr: e.matmul(Dn[s][:, 0:512], lhsT=actT[:, j, s * 128:(s + 1) * 128], rhs=w_[:, k, :],
                                                                                 start=(jr == 0 and k == 0), stop=(jr == 2 and k == 7)),
                                   [wn, 'actT'], ['Dn%d' % s])
                    for s in range(nsub):
                        DVE(lambda e, s=s, c=c: e.tensor_tensor(out=xt[:, s, c * 512:(c + 1) * 512], in0=Dn[s][:, 0:512],
                                                                in1=xt[:, s, c * 512:(c + 1) * 512], op=ALU.add), ['Dn%d' % s, 'x3'], ['x3'])
                r0 = t0 - QS
                STORE(y[r0:r0 + W, :].rearrange("(s p) d -> p s d", p=128), xt[:, 0:nsub, :], ['x3'], ['y'], 'ystore')
            S.barrier()
        S.emit()
    return nc


def _bf(a):
    return np.ascontiguousarray(a).astype(ml_dtypes.bfloat16)


def host_consts(cfg, h):
    L, QS, NS, NCC, NQT = cfg['L'], cfg['QS'], cfg['NS'], cfg['NCC'], cfg['NQT']
    B0 = QS // 64 if h == 0 else 0
    n = np.arange(NS)[None, :]
    rankc = np.zeros((NQT, 128, 2 * NS), np.float32)
    for il in range(NQT):
        t = QS - 128 + 128 * il + np.arange(128)[:, None]
        cur = t // 64
        C = (n <= cur) & (n >= B0)
        forced = (n == B0) | (n == cur) | (n == cur - 1)
        rankc[il, :, :NS] = 16.0 + 16.0 * forced
        rankc[il, :, NS:] = C
    c = (np.arange(NCC)[None, :] * 128 + np.arange(128)[:, None])
    valid = (c <= L // 16 - 2) & ((16 * c >= QS) if h == 0 else True)
    cc = c[:, :, None]
    lo = np.maximum(cc * 16, n[None] * 64)
    hi = np.minimum(cc * 16 + 32, (n[None] + 1) * 64)
    ovl = np.maximum(hi - lo, 0) / 16.0 * valid[:, :, None]
    cl = np.arange(128)[:, None, None]
    mm = np.arange(17)[None, :, None]
    ql = np.arange(128)[None, None, :]
    cmask = (16 * cl + 31 - ql <= 128 * mm)
    kl = np.arange(128)[:, None]
    qq = np.arange(128)[None, :]
    tri = np.stack([kl <= qq, kl > qq], axis=1)
    wexp = (np.arange(L)[None, :] // 64 == np.arange(NS)[:, None])
    pp = np.arange(128)
    bones = (pp[:, None] // 64 == pp[None, :] // 64) / 64.0
    flags = np.zeros((128, 2), np.float32)
    flags[:, :] = 1.0 if h == 1 else 0.0
    return dict(rankc=rankc, ovl=_bf(ovl), validc=valid.astype(np.float32), cmask=_bf(cmask), tri=_bf(tri), wexp=_bf(wexp),
                ident=_bf(np.eye(128)), bones=_bf(bones), flags=flags)


def host_weights(inp):
    f32 = np.float32
    w_in = np.asarray(inp['w_in'], f32)
    kvw = NKV * HD
    sizes = [D_RNN, D_RNN, NH * HD] + [kvw] * 6 + [3 * NH, D, D]
    offs = np.concatenate([[0], np.cumsum(sizes)])
    sec = {nm: w_in[:, offs[i]:offs[i + 1]] for i, nm in enumerate(
        ['rx', 'rg', 'q', 'kc', 'vc', 'ks', 'vs', 'kw', 'vw', 'ng', 'mr', 'ma'])}
    qcols = []
    for G_ in range(2):
        for r in range(4):
            for hd_ in (8 * G_ + r, 8 * G_ + 4 + r):
                qcols += list(range(hd_ * 64, hd_ * 64 + 64))
    win = np.concatenate([sec['rx'], sec['rg'], sec['q'][:, qcols], sec['kc'], sec['vc'], sec['ks'], sec['kw'],
                          sec['vs'], sec['vw'], sec['ng'], sec['mr'], sec['ma']], axis=1)
    assert win.shape[1] == N_IN

    def pkn(w, kp):
        K, N = w.shape
        return np.ascontiguousarray(w.reshape(K // kp, kp, N).transpose(1, 0, 2))
    out = {}
    out['f_win'] = pkn(win, 128)
    out['f_wro'] = pkn(np.asarray(inp['w_rnn_out'], f32), BW)
    out['f_wao'] = pkn(np.asarray(inp['w_attn_out'], f32), 128)
    out['f_wout'] = pkn(np.asarray(inp['w_out'], f32), 128)
    out['f_wup'] = pkn(np.asarray(inp['w_up'], f32), 128)
    out['f_wdn'] = pkn(np.asarray(inp['w_down'], f32), 128)
    out['f_rga'] = np.ascontiguousarray(np.asarray(inp['rg_w_a'], f32).transpose(1, 0, 2))
    out['f_rgx'] = np.ascontiguousarray(np.asarray(inp['rg_w_x'], f32).transpose(1, 0, 2))
    for nm, key in (('w1k', 'cmp_w1_k'), ('w1v', 'cmp_w1_v')):
        w1 = np.asarray(inp[key], f32).reshape(32, 64, 128).transpose(1, 0, 2)
        out['f_' + nm] = np.ascontiguousarray(np.concatenate([w1, w1], axis=0))
    out['f_w2k'] = np.asarray(inp['cmp_w2_k'], f32).reshape(128, 1, 64).copy()
    out['f_w2v'] = np.asarray(inp['cmp_w2_v'], f32).reshape(128, 1, 64).copy()
    for nm, key in (('pek', 'cmp_pe_k'), ('pev', 'cmp_pe_v')):
        pe = np.asarray(inp[key], f32).T
        out['f_' + nm] = np.ascontiguousarray(np.concatenate([pe, pe], axis=0).reshape(128, 1, 32))
    out['g1'] = np.ascontiguousarray(np.asarray(inp['norm1_g'], f32).reshape(8, 128).T)
    out['g2'] = np.ascontiguousarray(np.asarray(inp['norm2_g'], f32).reshape(8, 128).T)
    rn = np.zeros((BW, NB, 8), f32)
    cw = np.asarray(inp['rnn_conv_w'], f32)
    for k in range(4):
        rn[:, :, k] = cw[k].reshape(NB, BW).T
    for k, key in ((4, 'rnn_conv_b'), (5, 'rg_b_a'), (6, 'rg_b_x'), (7, 'lru_lambda')):
        rn[:, :, k] = np.asarray(inp[key], f32).reshape(NB, BW).T
    out['rnnc'] = rn
    fc = np.zeros((128, 48, 4), f32)
    fw = np.asarray(inp['ffn_conv_w'], f32)
    for k in range(3):
        fc[:, :, k] = fw[k].reshape(48, 128).T
    fc[:, :, 3] = np.asarray(inp['ffn_conv_b'], f32).reshape(48, 128).T
    out['ffnc'] = fc
    hg = np.zeros((128, 4), f32)
    hg[:, 0] = np.tile(np.asarray(inp['q_norm_g'], f32), 2)
    hg[:, 1] = np.tile(np.asarray(inp['k_cmp_norm_g'], f32), 2)
    hg[:, 2] = np.tile(np.asarray(inp['k_slc_norm_g'], f32), 2)
    hg[:, 3] = np.tile(np.asarray(inp['k_win_norm_g'], f32), 2)
    out['hgains'] = hg
    return out


_CACHE = {}


def run_cfg(cfg, inputs, x, debug=False, upto=3):
    B = x.shape[0]
    half = cfg['L'] - cfg['QS']
    key = (cfg['L'], cfg['QS'], debug)
    if key not in _CACHE:
        _CACHE[key] = build_program(cfg, debug, upto)
    nc = _CACHE[key]
    hw = host_weights(inputs)
    consts = [host_consts(cfg, 0), host_consts(cfg, 1)]
    in_maps = []
    for b in range(B):
        for h in range(2):
            xl = np.zeros((cfg['L'], D), np.float32)
            assert cfg['QS'] == half
            if h == 0:
                xl[cfg['QS']:] = x[b, :half]
            else:
                xl[:] = x[b]
            m = dict(hw)
            m.update(consts[h])
            m['xloc'] = xl
            in_maps.append(m)
    res = run_bass_kernel_spmd(nc, in_maps, core_ids=list(range(len(in_maps))))
    out = np.zeros((B, 2 * half, D), np.float32)
    for b in range(B):
        for h in range(2):
            out[b, h * half:(h + 1) * half] = res.results[2 * b + h]['y']
    return out, res


def kernel(**inputs):
    x = np.asarray(inputs['x'], np.float32)
    cfg = make_cfg(8192, 4096)
    out, _ = run_cfg(cfg, inputs, x)
    return out
```
